# Optimizing a Trainium2 kernel written in Bass

```python
import jax, jax.numpy as jnp
from jax import lax
import numpy as np

D_MODEL = 2048
BATCH = 2
SEQ = 16384
DEPTH = 2
DEC_BATCH = 16
DEC_SEQ = 32
PAST_LEN = 1024

CHUNK = 64
N_MIXERS = 2
N_FOX = (DEPTH + 1) // 2
N_SWA = DEPTH // 2
FOX_HEADS = 16
FOX_HD = D_MODEL // FOX_HEADS
FOX_QBLOCK = 128
FORGET_BIAS_INIT = 3.0
SWA_HEADS = 32
SWA_KV_HEADS = 4
SWA_GROUP = SWA_HEADS // SWA_KV_HEADS
SWA_HD = D_MODEL // SWA_HEADS
SWA_WINDOW = 128
SWA_WINDOW_CHUNKS = SWA_WINDOW // CHUNK
ROPE_THETA = 500000.0
ROPE_DIMS = SWA_HD // 4
D_FF = 4 * D_MODEL
RMS_EPS = 1e-6

kernel_name = "fox_swa_sink_hybrid_stream_step"


def rms_norm(x, g):
    x32 = x.astype(jnp.float32)
    y = x32 * lax.rsqrt(jnp.mean(x32 * x32, axis=-1, keepdims=True) + RMS_EPS)
    return y.astype(x.dtype) * g


def ada_params(c, w_mod, b_mod):
    m = jax.nn.silu(c) @ w_mod + b_mod
    return jnp.split(m[:, None, :], 6, axis=-1)


def modulate(h, shift, scale):
    return h * (1 + scale) + shift


def sq_relu_mlp(h, w_up, w_down):
    a = jax.nn.relu(h @ w_up)
    return (a * a) @ w_down


def partial_rope(x, pos):
    half = ROPE_DIMS // 2
    inv_freq = ROPE_THETA ** (-jnp.arange(half, dtype=jnp.float32) * 2.0 / ROPE_DIMS)
    ang = pos.astype(jnp.float32)[:, None] * inv_freq[None, :]
    cos = jnp.cos(ang)[:, None, :]
    sin = jnp.sin(ang)[:, None, :]
    x32 = x.astype(jnp.float32)
    x1 = x32[..., :half]
    x2 = x32[..., half:ROPE_DIMS]
    rot = jnp.concatenate([x1 * cos - x2 * sin, x2 * cos + x1 * sin], axis=-1).astype(x.dtype)
    return jnp.concatenate([rot, x[..., ROPE_DIMS:]], axis=-1)


def fox_project(h, w_in, b_f):
    b, t, _ = h.shape
    d = D_MODEL
    proj = h @ w_in
    q = proj[..., :d].reshape(b, t, FOX_HEADS, FOX_HD)
    k = proj[..., d:2 * d].reshape(b, t, FOX_HEADS, FOX_HD)
    v = proj[..., 2 * d:3 * d].reshape(b, t, FOX_HEADS, FOX_HD)
    logf = jax.nn.log_sigmoid((proj[..., 3 * d:] + b_f).astype(jnp.float32))
    return q, k, v, logf


def fox_attend(q, k, v, cq, ck, q_pos, k_pos):
    s = jnp.einsum('bqhd,bkhd->bhqk', q, k).astype(jnp.float32) * (FOX_HD ** -0.5)
    s = s + jnp.swapaxes(cq, 1, 2)[..., :, None] - jnp.swapaxes(ck, 1, 2)[..., None, :]
    s = jnp.where(k_pos[None, :] <= q_pos[:, None], s, -jnp.inf)
    p = jax.nn.softmax(s, axis=-1)
    return jnp.einsum('bhqk,bkhd->bqhd', p.astype(v.dtype), v)


def fox_prompt(q, k, v, logf):
    b, s, h, d = q.shape
    c = jnp.cumsum(logf, axis=1)
    pos = jnp.arange(s)
    nb = s // FOX_QBLOCK
    qb = q.reshape(b, nb, FOX_QBLOCK, h, d).swapaxes(0, 1)
    cb = c.reshape(b, nb, FOX_QBLOCK, h).swapaxes(0, 1)
    pb = pos.reshape(nb, FOX_QBLOCK)
    out = lax.map(lambda a: fox_attend(a[0], k, v, a[1], c, a[2], pos), (qb, cb, pb))
    return out.swapaxes(0, 1).reshape(b, s, h * d)


def fox_sample(q, k, v, logf, ck_cache, cv_cache, clogf_cache):
    b, t, h, d = q.shape
    p_len = ck_cache.shape[1]
    k_all = jnp.concatenate([ck_cache.astype(k.dtype), k], axis=1)
    v_all = jnp.concatenate([cv_cache.astype(v.dtype), v], axis=1)
    c_all = jnp.cumsum(jnp.concatenate([clogf_cache.astype(jnp.float32), logf], axis=1), axis=1)
    k_pos = jnp.arange(p_len + t)
    q_pos = p_len + jnp.arange(t)
    out = fox_attend(q, k_all, v_all, c_all[:, p_len:], c_all, q_pos, k_pos)
    return out.reshape(b, t, h * d)


def swa_project(h, w_in, pos):
    b, t, _ = h.shape
    qd = SWA_HEADS * SWA_HD
    kvd = SWA_KV_HEADS * SWA_HD
    proj = h @ w_in
    q = partial_rope(proj[..., :qd].reshape(b, t, SWA_HEADS, SWA_HD), pos)
    k = partial_rope(proj[..., qd:qd + kvd].reshape(b, t, SWA_KV_HEADS, SWA_HD), pos)
    v = proj[..., qd + kvd:].reshape(b, t, SWA_KV_HEADS, SWA_HD)
    return q, k, v


def sink_attend(q, k, v, mask, sinks):
    s = jnp.einsum('bnqhgd,bnshd->bnhgqs', q, k).astype(jnp.float32) * (SWA_HD ** -0.5)
    s = jnp.where(mask[None, :, None, None], s, -jnp.inf)
    sink = sinks.astype(jnp.float32)[None, None, :, :, None, None]
    m = jnp.maximum(jnp.max(s, axis=-1, keepdims=True), sink)
    p = jnp.exp(s - m)
    p = p / (jnp.sum(p, axis=-1, keepdims=True) + jnp.exp(sink - m))
    return jnp.einsum('bnhgqs,bnshd->bnqhgd', p.astype(v.dtype), v)


def swa_prompt(q, k, v, sinks):
    b, s, _, _ = q.shape
    nc = s // CHUNK
    w = SWA_WINDOW_CHUNKS
    qc = q.reshape(b, nc, CHUNK, SWA_KV_HEADS, SWA_GROUP, SWA_HD)
    pad = jnp.zeros((b, w, CHUNK, SWA_KV_HEADS, SWA_HD), k.dtype)
    kp = jnp.concatenate([pad, k.reshape(b, nc, CHUNK, SWA_KV_HEADS, SWA_HD)], axis=1)
    vp = jnp.concatenate([pad.astype(v.dtype), v.reshape(b, nc, CHUNK, SWA_KV_HEADS, SWA_HD)], axis=1)
    kb = jnp.concatenate([kp[:, j:j + nc] for j in range(w + 1)], axis=2)
    vb = jnp.concatenate([vp[:, j:j + nc] for j in range(w + 1)], axis=2)
    offs = jnp.repeat(jnp.arange(w + 1), CHUNK)
    kchunk = jnp.arange(nc)[:, None] - w + offs[None, :]
    mask = (kchunk >= 0)[:, None, :]
    out = sink_attend(qc, kb, vb, mask, sinks.reshape(SWA_KV_HEADS, SWA_GROUP))
    return out.reshape(b, s, D_MODEL)


def swa_sample(q, k, v, sinks, ck_cache, cv_cache, past_len):
    b, t, _, _ = q.shape
    buf = ck_cache.shape[1]
    k_all = jnp.concatenate([ck_cache.astype(k.dtype), k], axis=1)
    v_all = jnp.concatenate([cv_cache.astype(v.dtype), v], axis=1)
    q_pos = past_len + jnp.arange(t)
    k_pos = jnp.concatenate([past_len - buf + jnp.arange(buf), q_pos])
    qch = q_pos // CHUNK
    kch = k_pos // CHUNK
    mask = (kch[None, :] <= qch[:, None]) & (kch[None, :] >= qch[:, None] - SWA_WINDOW_CHUNKS)
    out = sink_attend(q.reshape(b, 1, t, SWA_KV_HEADS, SWA_GROUP, SWA_HD), k_all[:, None], v_all[:, None],
                      mask[None], sinks.reshape(SWA_KV_HEADS, SWA_GROUP))
    return out.reshape(b, t, D_MODEL), k_all[:, -buf:], v_all[:, -buf:]


def setup_inputs(seed: int = 0) -> dict:
    key = jax.random.key(seed)
    ks = jax.random.split(key, 24)
    n = jax.random.normal
    d = D_MODEL
    wbuf = min(SWA_WINDOW, PAST_LEN)
    fox_in = 3 * d + FOX_HEADS
    swa_in = SWA_HEADS * SWA_HD + 2 * SWA_KV_HEADS * SWA_HD
    return {
        "x_prompt": n(ks[0], (BATCH, SEQ, d), jnp.float32),
        "x_sample": n(ks[1], (DEC_BATCH, DEC_SEQ, d), jnp.float32),
        "c_prompt": n(ks[2], (BATCH, d), jnp.float32),
        "c_sample": n(ks[3], (DEC_BATCH, d), jnp.float32),
        "cache_fox_k": n(ks[4], (N_FOX, DEC_BATCH, PAST_LEN, FOX_HEADS, FOX_HD), jnp.float32),
        "cache_fox_v": n(ks[5], (N_FOX, DEC_BATCH, PAST_LEN, FOX_HEADS, FOX_HD), jnp.float32),
        "cache_fox_logf": jax.nn.log_sigmoid(FORGET_BIAS_INIT + n(ks[6], (N_FOX, DEC_BATCH, PAST_LEN, FOX_HEADS), jnp.float32)),
        "cache_swa_k": n(ks[7], (N_SWA, DEC_BATCH, wbuf, SWA_KV_HEADS, SWA_HD), jnp.float32),
        "cache_swa_v": n(ks[8], (N_SWA, DEC_BATCH, wbuf, SWA_KV_HEADS, SWA_HD), jnp.float32),
        "ada_w": n(ks[9], (DEPTH, d, 6 * d), jnp.float32) * (0.5 * d ** -0.5),
        "ada_b": n(ks[10], (DEPTH, 6 * d), jnp.float32) * 0.02,
        "norm_mix_g": 1.0 + 0.05 * n(ks[11], (DEPTH, d), jnp.float32),
        "norm_ffn_g": 1.0 + 0.05 * n(ks[12], (DEPTH, d), jnp.float32),
        "fox_w_in": n(ks[13], (N_FOX, d, fox_in), jnp.float32) * d ** -0.5,
        "fox_b_f": FORGET_BIAS_INIT + 0.1 * n(ks[14], (N_FOX, FOX_HEADS), jnp.float32),
        "fox_w_out": n(ks[15], (N_FOX, d, d), jnp.float32) * d ** -0.5,
        "swa_w_in": n(ks[16], (N_SWA, d, swa_in), jnp.float32) * d ** -0.5,
        "swa_sinks": 0.5 * n(ks[17], (N_SWA, SWA_HEADS), jnp.float32),
        "swa_w_out": n(ks[18], (N_SWA, d, d), jnp.float32) * d ** -0.5,
        "ffn_w_up": n(ks[19], (DEPTH, d, D_FF), jnp.float32) * d ** -0.5,
        "ffn_w_down": n(ks[20], (DEPTH, D_FF, d), jnp.float32) * D_FF ** -0.5,
        "final_g": 1.0 + 0.05 * n(ks[21], (d,), jnp.float32),
    }


def reference(x_prompt, x_sample, c_prompt, c_sample, cache_fox_k, cache_fox_v, cache_fox_logf,
              cache_swa_k, cache_swa_v, ada_w, ada_b, norm_mix_g, norm_ffn_g, fox_w_in, fox_b_f,
              fox_w_out, swa_w_in, swa_sinks, swa_w_out, ffn_w_up, ffn_w_down, final_g):
    past_len = cache_fox_k.shape[2]
    pos_p = jnp.arange(x_prompt.shape[1])
    pos_s = past_len + jnp.arange(x_sample.shape[1])
    xp, xs = x_prompt, x_sample
    fkp, fvp, flp, fks, fvs, fls = [], [], [], [], [], []
    skp, svp, sks, svs = [], [], [], []
    for i in range(DEPTH):
        mp = ada_params(c_prompt, ada_w[i], ada_b[i])
        ms = ada_params(c_sample, ada_w[i], ada_b[i])
        hp = modulate(rms_norm(xp, norm_mix_g[i]), mp[0], mp[1])
        hs = modulate(rms_norm(xs, norm_mix_g[i]), ms[0], ms[1])
        j = i // N_MIXERS
        if i % N_MIXERS == 0:
            q, k, v, lf = fox_project(hp, fox_w_in[j], fox_b_f[j])
            op = fox_prompt(q, k, v, lf) @ fox_w_out[j]
            fkp.append(k); fvp.append(v); flp.append(lf)
            q, k, v, lf = fox_project(hs, fox_w_in[j], fox_b_f[j])
            os_ = fox_sample(q, k, v, lf, cache_fox_k[j], cache_fox_v[j], cache_fox_logf[j]) @ fox_w_out[j]
            fks.append(k); fvs.append(v); fls.append(lf)
        else:
            buf = cache_swa_k.shape[2]
            q, k, v = swa_project(hp, swa_w_in[j], pos_p)
            op = swa_prompt(q, k, v, swa_sinks[j]) @ swa_w_out[j]
            skp.append(k[:, -buf:]); svp.append(v[:, -buf:])
            q, k, v = swa_project(hs, swa_w_in[j], pos_s)
            o, kb, vb = swa_sample(q, k, v, swa_sinks[j], cache_swa_k[j], cache_swa_v[j], past_len)
            os_ = o @ swa_w_out[j]
            sks.append(kb); svs.append(vb)
        xp = xp + mp[2] * op
        xs = xs + ms[2] * os_
        hp = modulate(rms_norm(xp, norm_ffn_g[i]), mp[3], mp[4])
        hs = modulate(rms_norm(xs, norm_ffn_g[i]), ms[3], ms[4])
        xp = xp + mp[5] * sq_relu_mlp(hp, ffn_w_up[i], ffn_w_down[i])
        xs = xs + ms[5] * sq_relu_mlp(hs, ffn_w_up[i], ffn_w_down[i])
    y_prompt = rms_norm(xp, final_g)
    y_sample = rms_norm(xs, final_g)
    return (y_prompt, y_sample,
            jnp.stack(fkp), jnp.stack(fvp), jnp.stack(flp),
            jnp.stack(fks), jnp.stack(fvs), jnp.stack(fls),
            jnp.stack(skp), jnp.stack(svp),
            jnp.stack(sks), jnp.stack(svs))
```

```python
import contextlib
import numpy as np
import concourse.bass as bass
import concourse.mybir as mybir
from concourse.bass_utils import run_bass_kernel_spmd

F32 = mybir.dt.float32
BF16 = mybir.dt.bfloat16
AF = mybir.ActivationFunctionType
ALU = mybir.AluOpType
P = 128
NEG = -30000.0


class Buf:
    __slots__ = ("w", "r")

    def __init__(self):
        self.w = None
        self.r = {}


class Trk:
    ENG = ("pe", "dve", "act", "pool", "sp")

    def __init__(self):
        self.ops = {e: [] for e in self.ENG}
        self.seq = {e: 0 for e in self.ENG}
        self.seen = {e: {} for e in self.ENG}
        self.dcnt = {}

    def op(self, eng, fn, reads=(), writes=(), dma=None):
        waits = {}
        seen = self.seen[eng]

        def need(tok):
            if tok is None:
                return
            k, v = tok
            if eng == "pe" and k == "pe":
                return
            if seen.get(k, 0) >= v:
                return
            if waits.get(k, 0) < v:
                waits[k] = v

        for b in reads:
            need(b.w)
        for b in writes:
            need(b.w)
            for k, v in b.r.items():
                need((k, v))
        if dma is not None:
            c = self.dcnt.get(dma, 0)
            if eng == "pool" and c > 0:
                need((dma, c))
            c += 16
            self.dcnt[dma] = c
            tok = (dma, c)
            inc = 16
        else:
            self.seq[eng] += 1
            tok = (eng, self.seq[eng])
            inc = 1
        for k, v in waits.items():
            seen[k] = v
        for b in reads:
            if b.r.get(tok[0], 0) < tok[1]:
                b.r[tok[0]] = tok[1]
        for b in writes:
            b.w = tok
            b.r = {}
        self.ops[eng].append((list(waits.items()), fn, tok[0], inc))

    def barrier(self):
        for e in self.ENG:
            waits = []
            for k in self.ENG:
                if k != e and self.seq[k] > self.seen[e].get(k, 0):
                    waits.append((k, self.seq[k]))
                    self.seen[e][k] = self.seq[k]
            for k, v in self.dcnt.items():
                if k.startswith("cast"):
                    continue
                if v > self.seen[e].get(k, 0):
                    waits.append((k, v))
                    self.seen[e][k] = v
            if waits:
                self.ops[e].append((waits, None, None, 0))

    def emit(self, nc, sems):
        with nc.Block() as block:
            for ename, deco in (("pe", block.tensor), ("dve", block.vector), ("act", block.scalar),
                                ("pool", block.gpsimd), ("sp", block.sync)):
                ops = self.ops[ename]

                def body(e, ops=ops):
                    for waits, fn, sk, inc in ops:
                        for k, v in waits:
                            e.wait_ge(sems[k], v)
                        if fn is not None:
                            fn(e).then_inc(sems[sk], inc)

                deco(body)


def make_cfg(D, S, T, PAST, WB):
    c = dict(D=D, S=S, T=T, PAST=PAST, WB=WB)
    c["KC"] = D // P
    c["NB"] = S // P
    c["OWN"] = c["NB"] // 4
    c["NO"] = c["NB"] - c["OWN"] - 1
    c["FH"] = D // 128
    c["SH"] = D // 64
    c["KV"] = 4
    c["GRP"] = c["SH"] // 4
    c["DFF"] = 4 * D
    c["HC"] = c["DFF"] // P
    c["PB"] = PAST // P
    c["L"] = 2
    c["QKV"] = D + 2 * 4 * 64
    assert (c["OWN"] + 1) % 3 == 0 and c["NB"] % 4 == 0 and c["NB"] <= 128
    gl = [(0, 1)]
    b = 1
    while b <= c["OWN"]:
        nb = min(4, c["OWN"] + 1 - b)
        gl.append((b, nb))
        b += nb
    c["GL"] = gl
    return c


class Builder:
    def __init__(self, cfg):
        self.c = cfg
        self.nc = bass.Bass("TRN2", target_bir_lowering=False)
        self.t = Trk()
        self.es = contextlib.ExitStack()
        self.dram = {}
        self.nid = 0

    def din(self, name, shape, dt=F32):
        self.dram[name] = self.nc.dram_tensor(name, list(shape), dt, kind="ExternalInput").ap()
        return self.dram[name]

    def dout(self, name, shape, dt=F32):
        self.dram[name] = self.nc.dram_tensor(name, list(shape), dt, kind="ExternalOutput").ap()
        return self.dram[name]

    def dscr(self, name, shape, dt=BF16):
        self.dram[name] = self.nc.dram_tensor(name, list(shape), dt, kind="Internal").ap()
        return self.dram[name]

    def sb(self, stack, shape, dt, name=None):
        self.nid += 1
        return stack.enter_context(self.nc.sbuf_tensor(f"{name or 't'}_{self.nid}", list(shape), dt))

    def op(self, eng, fn, reads=(), writes=(), dma=None):
        self.t.op(eng, fn, reads, writes, dma)

    def dma(self, out, in_, reads=(), writes=(), key=None, eng="sp"):
        self.op(eng, lambda e: e.dma_start(out=out, in_=in_), reads, writes, dma=key)

    def declare(self):
        c = self.c
        D, KC, NB, OWN, FH, SH, KV, T, L = c["D"], c["KC"], c["NB"], c["OWN"], c["FH"], c["SH"], c["KV"], c["T"], c["L"]
        di = self.din
        di("xloc", [NB * P, D]); di("xs", [2 * T, D]); di("cT", [P, KC, 3])
        di("ada_w", [L, D, 6 * D]); di("ada_bT", [P, L, 6 * KC])
        di("gmixT", [P, L, KC]); di("gffnT", [P, L, KC]); di("gfinT", [P, KC])
        di("fox_w_in", [D, 3 * D + FH]); di("fox_bfb", [P, FH]); di("fox_w_out", [D, D])
        di("swa_w_in", [D, c["QKV"]]); di("sinkb", [P, SH]); di("swa_w_out", [D, D])
        di("w_up", [L, D, c["DFF"]]); di("w_dn", [L, c["DFF"], D])
        di("cfk", [2, c["PAST"], D]); di("cfv", [2, c["PAST"], D]); di("cfl", [2, c["PAST"], FH])
        di("csk", [2, c["WB"], KV * 64]); di("csv", [2, c["WB"], KV * 64])
        di("cf", [P, 6 * P]); di("bmb", [P, len(c["GL"]), NB]); di("halom", [P, 1])
        di("ropec", [(OWN + 1) * P, SH * 8]); di("ropes", [(OWN + 1) * P, SH * 8])
        di("ropecs", [2 * T, SH * 8]); di("ropess", [2 * T, SH * 8])
        do = self.dout
        do("y", [OWN * P, D]); do("ys", [2 * T, D])
        do("fk", [OWN * P, D]); do("fv", [OWN * P, D]); do("fl", [OWN * P, FH])
        do("fks", [2 * T, D]); do("fvs", [2 * T, D]); do("fls", [2 * T, FH])
        do("sk", [P, KV * 64]); do("sv", [P, KV * 64])
        do("sks", [2, c["WB"], KV * 64]); do("svs", [2, c["WB"], KV * 64])
        ds = self.dscr
        ds("wb_foxin", [D, 3 * D + FH]); ds("wb_foxout", [D, D]); ds("wb_swain", [D, c["QKV"]]); ds("wb_swaout", [D, D])
        ds("wb_up", [L, D, c["DFF"]]); ds("wb_dn", [L, KC, P, c["HC"], P])
        ds("QT", [FH, P, (OWN + 1) * P]); ds("KT", [FH, P, NB * P]); ds("VP", [FH, P, NB, 128]); ds("OT", [FH, P, (OWN + 1) * P])
        self.wbufs = {}

    def cast_weights(self):
        c = self.c
        d = self.dram
        kk = [0]

        def cast(dst, src, name):
            b = Buf()
            self.wbufs.setdefault(name, []).append(b)
            key = f"cast{kk[0] % 4}"
            kk[0] += 1
            self.dma(dst, src, writes=[b], key=key, eng="pool")

        def cast2d(dname, sname, ncols, name, l=None):
            for c0 in range(0, ncols, 2048):
                c1 = min(ncols, c0 + 2048)
                if l is None:
                    cast(d[dname][:, c0:c1], d[sname][:, c0:c1], name)
                else:
                    cast(d[dname][l, :, c0:c1], d[sname][l, :, c0:c1], name)

        cast2d("wb_foxin", "fox_w_in", 3 * c["D"] + c["FH"], "foxin")
        cast2d("wb_foxout", "fox_w_out", c["D"], "foxout")
        for l in range(c["L"]):
            cast2d("wb_up", "w_up", c["DFF"], f"up{l}", l)
            src = d["w_dn"][l].rearrange("(hc p) (dc n) -> dc p hc n", p=P, n=P)
            for dc in range(c["KC"]):
                cast(d["wb_dn"][l, dc], src[dc], f"dn{l}")
            if l == 0:
                cast2d("wb_swain", "swa_w_in", c["QKV"], "swain")
                cast2d("wb_swaout", "swa_w_out", c["D"], "swaout")

    class WPipe:
        def __init__(self, bld, slots, sbufs, loads, depth=3):
            self.b, self.slots, self.sbufs, self.loads, self.depth = bld, slots, sbufs, loads, depth
            self.issued = 0

        def get(self, i):
            while self.issued < len(self.loads) and self.issued < i + self.depth:
                j = self.issued
                s = j % len(self.slots)
                self.loads[j](self.slots[s], self.sbufs[s], f"w{s}")
                self.issued += 1
            s = i % len(self.slots)
            return self.slots[s], self.sbufs[s]

    def wload_cols(self, dname, name, c0, ncols, l=None):
        def f(slot, sbuf, key):
            src = self.dram[dname] if l is None else self.dram[dname][l]
            srcv = src.rearrange("(kc p) n -> p kc n", p=P)[:, :, c0:c0 + ncols]
            dst = slot[:, 0:self.c["KC"] * 512].rearrange("p (kc n) -> p kc n", n=512)[:, :, 0:ncols]
            self.dma(dst, srcv, reads=self.wbufs[name], writes=[sbuf], key=key)
        return f

    def wload_dn(self, l, dc, half):
        def f(slot, sbuf, key):
            HH = self.c["HC"] // 2
            self.dma(slot[:, 0:HH * P], self.dram["wb_dn"][l, dc][:, half * HH:(half + 1) * HH, :].rearrange("p hc n -> p (hc n)"),
                     reads=self.wbufs[f"dn{l}"], writes=[sbuf], key=key)
        return f

    def build(self):
        c = self.c
        nc = self.nc
        self.declare()
        D, KC, NB, OWN, FH, SH, KV, T, L = c["D"], c["KC"], c["NB"], c["OWN"], c["FH"], c["SH"], c["KV"], c["T"], c["L"]
        HC = c["HC"]
        d = self.dram
        with self.es as es:
            self.ps = [es.enter_context(nc.psum_tensor(f"ps{i}", [P, 512], F32)) for i in range(8)]
            self.psb = [Buf() for _ in range(8)]
            g = es
            cf = self.sb(g, [P, 6 * P], F32, "cf"); self.cfB = Buf()
            self.dma(cf[:], d["cf"][:, :], writes=[self.cfB], key="c_cf")
            self.ident = cf[:, 0:P]; self.Ule = cf[:, P:2 * P]; self.Lgt = cf[:, 2 * P:3 * P]
            self.ones32 = cf[:, 3 * P:4 * P]; self.E127 = cf[:, 4 * P:5 * P]; self.SLp = cf[:, 5 * P:6 * P]
            cb = self.sb(g, [P, 3 * P], BF16, "cb"); self.cbB = Buf()
            cst = self.sb(g, [P, 4], F32, "cst"); self.cstB = Buf()
            self.op("dve", lambda e: e.memset(cst[:, 0:1], 1e-6), writes=[self.cstB])
            self.op("dve", lambda e: e.memset(cst[:, 1:2], 1.0), reads=[], writes=[self.cstB])
            self.op("dve", lambda e: e.memset(cst[:, 2:3], 0.0), writes=[self.cstB])
            self.eps = cst[:, 0:1]; self.one = cst[:, 1:2]; self.zero = cst[:, 2:3]
            self.op("dve", lambda e: e.tensor_scalar(out=cb[:, 0:P], in0=self.ones32, scalar1=1.0 / D, scalar2=None, op0=ALU.mult),
                    reads=[self.cfB], writes=[self.cbB])
            self.op("dve", lambda e: e.tensor_copy(out=cb[:, P:2 * P], in_=self.Ule), reads=[self.cfB], writes=[self.cbB])
            self.op("dve", lambda e: e.tensor_copy(out=cb[:, 2 * P:3 * P], in_=self.ones32), reads=[self.cfB], writes=[self.cbB])
            self.onesD = cb[:, 0:P]; self.tri = cb[:, P:2 * P]; self.ones16 = cb[:, 2 * P:3 * P]
            small = {}
            for nm, shp in (("ada_bT", [P, L, 6 * KC]), ("gmixT", [P, L, KC]), ("gffnT", [P, L, KC]), ("gfinT", [P, KC]),
                            ("fox_bfb", [P, FH]), ("sinkb", [P, SH]), ("halom", [P, 1]), ("cT", [P, KC, 3])):
                tl = self.sb(g, shp, F32, nm)
                bb = Buf()
                self.dma(tl[:], d[nm][:], writes=[bb], key="c_" + nm)
                small[nm] = (tl, bb)
            self.small = small
            self.modt = self.sb(g, [P, L, 6 * KC, 3], F32, "mod"); self.modB = Buf()
            self.A1 = self.sb(g, [P, L, KC, 3], F32, "A1"); self.A2 = self.sb(g, [P, L, KC, 3], F32, "A2")
            self.wsb = [Buf() for _ in range(3)]
            import os
            stop = int(os.environ.get("KSTOP", "9"))
            if stop >= 1:
                self.cast_weights()
            if stop >= 2:
                self.phase_ada()
            self.t.barrier()
            with contextlib.ExitStack() as ph12:
                self.LF = self.sb(ph12, [P, FH, NB], F32, "LF"); self.LFB = Buf()
                with contextlib.ExitStack() as ph:
                    self.wslots = [self.sb(ph, [P, KC * 512], BF16, f"w{i}") for i in range(3)]
                    if stop >= 3:
                        self.phase1(ph)
                    self.t.barrier()
                with contextlib.ExitStack() as ph:
                    if stop >= 4:
                        self.phase2(ph)
                    self.t.barrier()
            with contextlib.ExitStack() as ph:
                self.wslots = [self.sb(ph, [P, KC * 512], BF16, f"w{i}") for i in range(3)]
                if stop >= 5:
                    self.phase3(ph, True)
                self.t.barrier()
            with contextlib.ExitStack() as ph:
                self.wslots = [self.sb(ph, [P, KC * 512], BF16, f"w{i}") for i in range(3)]
                if stop >= 6:
                    self.phase3(ph, False)
                self.t.barrier()
            keys = list(Trk.ENG) + list(self.t.dcnt.keys())
            with nc.cleanup_on_exit():
                sems = {k: nc.alloc_semaphore(f"s_{k}") for k in keys}
                for k in keys:
                    nc.gpsimd.sem_clear(sems[k])
                nc.all_engine_barrier()
                self.t.emit(nc, sems)
        return nc

    def phase_ada(self):
        c = self.c
        nc = self.nc
        D, KC, L = c["D"], c["KC"], c["L"]
        d = self.dram
        cT, cTb = self.small["cT"]
        with contextlib.ExitStack() as ph:
            sc = self.sb(ph, [P, KC, 3], F32, "silu"); scB = Buf()
            sg = self.sb(ph, [P, KC, 3], F32, "sig")
            self.op("act", lambda e: e.activation(out=sg[:], in_=cT[:], func=AF.Sigmoid), reads=[cTb], writes=[scB])
            self.op("dve", lambda e: e.tensor_tensor(out=sc[:], in0=sg[:], in1=cT[:], op=ALU.mult), reads=[cTb, scB], writes=[scB])
            wa = [self.sb(ph, [P, KC, 512], F32, f"wa{i}") for i in range(2)]
            waB = [Buf(), Buf()]
            ng = 6 * D // 512
            bT, bTb = self.small["ada_bT"]
            pb = 7
            for l in range(L):
                for gi in range(ng):
                    s = (l * ng + gi) % 2
                    self.dma(wa[s][:], d["ada_w"][l].rearrange("(kc p) n -> p kc n", p=P)[:, :, gi * 512:(gi + 1) * 512],
                             writes=[waB[s]], key=f"wa{s}")
                    for j in range(4):
                        ch = gi * 4 + j
                        for kc in range(KC):
                            self.op("pe", lambda e, s=s, j=j, kc=kc, ch=ch: e.matmul(
                                out=self.ps[pb][:, ch * 3:ch * 3 + 3], lhsT=wa[s][:, kc, j * P:(j + 1) * P], rhs=sc[:, kc, :],
                                start=(kc == 0), stop=(kc == KC - 1)), reads=[waB[s], scB], writes=[self.psb[pb]])
                for cd in range(3):
                    self.op("dve", lambda e, l=l, cd=cd: e.tensor_tensor(
                        out=self.modt[:, l, :, cd], in0=self.ps[pb][:, 0:6 * KC * 3].rearrange("p (j c) -> p j c", c=3)[:, :, cd],
                        in1=bT[:, l, :], op=ALU.add), reads=[self.psb[pb], bTb], writes=[self.modB])
            gm, gmb = self.small["gmixT"]
            gf, gfb = self.small["gffnT"]
            for l in range(L):
                for cd in range(3):
                    self.op("dve", lambda e, l=l, cd=cd: e.scalar_tensor_tensor(
                        out=self.A1[:, l, :, cd], in0=self.modt[:, l, KC:2 * KC, cd], scalar=1.0, in1=gm[:, l, :],
                        op0=ALU.add, op1=ALU.mult), reads=[self.modB, gmb], writes=[self.modB])
                    self.op("dve", lambda e, l=l, cd=cd: e.scalar_tensor_tensor(
                        out=self.A2[:, l, :, cd], in0=self.modt[:, l, 4 * KC:5 * KC, cd], scalar=1.0, in1=gf[:, l, :],
                        op0=ALU.add, op1=ALU.mult), reads=[self.modB, gfb], writes=[self.modB])

    def modcol(self, l, kind, kc, cd):
        KC = self.c["KC"]
        if kind == "A1":
            return self.A1[:, l, kc, cd:cd + 1]
        if kind == "A2":
            return self.A2[:, l, kc, cd:cd + 1]
        off = {"B1": 0, "G1": 2, "B2": 3, "G2": 5}[kind]
        return self.modt[:, l, off * KC + kc, cd:cd + 1]

    def load_xT(self, xT, xTB, src_rows_fn, nblk, xin, xinB, rows=P):
        KC = self.c["KC"]
        for b in range(nblk):
            s = self.xcnt % 2
            self.xcnt += 1
            self.dma(xin[s][0:rows, :], src_rows_fn(b), writes=[xinB[s]], key=f"xin{s}")
            for k0 in range(0, KC, 4):
                pb = self.tcnt % 2
                self.tcnt += 1
                nk = min(4, KC - k0)
                for kk in range(nk):
                    kc = k0 + kk
                    self.op("pe", lambda e, s=s, kc=kc, kk=kk, pb=pb: e.transpose(
                        out=self.ps[pb][:, kk * P:kk * P + rows], in_=xin[s][0:rows, kc * P:(kc + 1) * P], identity=self.ident[0:rows, 0:rows]),
                        reads=[xinB[s], self.cfB], writes=[self.psb[pb]])
                eng = "act" if (self.tcnt % 2) else "dve"
                if eng == "act":
                    self.op("act", lambda e, k0=k0, nk=nk, pb=pb, b=b: e.activation(
                        out=xT[:, k0:k0 + nk, b * rows:(b + 1) * rows], in_=self.ps[pb][:, 0:nk * P].rearrange("p (k n) -> p k n", n=P)[:, :, 0:rows],
                        func=AF.Copy), reads=[self.psb[pb]], writes=[xTB])
                else:
                    self.op("dve", lambda e, k0=k0, nk=nk, pb=pb, b=b: e.tensor_copy(
                        out=xT[:, k0:k0 + nk, b * rows:(b + 1) * rows], in_=self.ps[pb][:, 0:nk * P].rearrange("p (k n) -> p k n", n=P)[:, :, 0:rows]),
                        reads=[self.psb[pb]], writes=[xTB])

    def norm_mod(self, xT, xTB, hT, hTB, N, segs, Afn, Bfn, sq, sqB, rs, rsB, xr, xrB, out_f32=None):
        KC = self.c["KC"]
        pb = 2
        for kc in range(KC):
            s = kc % 2
            self.op("pool", lambda e, kc=kc, s=s: e.tensor_tensor(out=sq[s][:, 0:N], in0=xT[:, kc, 0:N], in1=xT[:, kc, 0:N], op=ALU.mult),
                    reads=[xTB], writes=[sqB[s]])
            self.op("pe", lambda e, kc=kc, s=s: e.matmul(out=self.ps[pb][:, 0:N], lhsT=self.onesD, rhs=sq[s][:, 0:N],
                                                        start=(kc == 0), stop=(kc == KC - 1)),
                    reads=[sqB[s], self.cbB], writes=[self.psb[pb]])
        self.op("act", lambda e: e.activation(out=rs[:, 0:N], in_=self.ps[pb][:, 0:N], func=AF.Sqrt, bias=self.eps, scale=1.0),
                reads=[self.psb[pb], self.cstB], writes=[rsB])
        self.op("dve", lambda e: e.reciprocal(out=rs[:, 0:N], in_=rs[:, 0:N]), reads=[rsB], writes=[rsB])
        for kc in range(KC):
            s = kc % 2
            self.op("dve", lambda e, kc=kc, s=s: e.tensor_tensor(out=xr[s][:, 0:N], in0=xT[:, kc, 0:N], in1=rs[:, 0:N], op=ALU.mult),
                    reads=[xTB, rsB], writes=[xrB[s]])
            for (c0, c1, cd) in segs:
                dst = hT if out_f32 is None else out_f32
                self.op("act", lambda e, kc=kc, s=s, c0=c0, c1=c1, cd=cd, dst=dst: e.activation(
                    out=dst[:, kc, c0:c1], in_=xr[s][:, c0:c1], func=AF.Identity, bias=Bfn(kc, cd), scale=Afn(kc, cd)),
                    reads=[xrB[s], self.modB], writes=[hTB])

    def phase1(self, ph):
        c = self.c
        D, KC, NB, OWN, FH = c["D"], c["KC"], c["NB"], c["OWN"], c["FH"]
        d = self.dram
        self.xcnt = 0
        self.tcnt = 0
        xin = [self.sb(ph, [P, D], F32, f"xin{i}") for i in range(2)]; xinB = [Buf(), Buf()]
        xT = self.sb(ph, [P, KC, 512], F32, "xT"); xTB = Buf()
        hT = [self.sb(ph, [P, KC, 512], BF16, f"hT{i}") for i in range(2)]; hTB = [Buf(), Buf()]
        sq = [self.sb(ph, [P, 512], BF16, f"sq{i}") for i in range(2)]; sqB = [Buf(), Buf()]
        xr = [self.sb(ph, [P, 512], F32, f"xr{i}") for i in range(2)]; xrB = [Buf(), Buf()]
        rs = self.sb(ph, [P, 512], F32, "rs"); rsB = Buf()
        stg = [self.sb(ph, [P, 512], BF16, f"stg{i}") for i in range(4)]; stgB = [Buf() for _ in range(4)]
        f32s = [self.sb(ph, [P, 512], F32, f"f32s{i}") for i in range(4)]; f32B = [Buf() for _ in range(4)]
        vps = [self.sb(ph, [P, 512], BF16, f"vps{i}") for i in range(2)]; vpB = [Buf(), Buf()]
        wf = self.sb(ph, [P, KC, FH], BF16, "wf"); wfB = Buf()
        lz = [self.sb(ph, [P, FH], F32, f"lz{i}") for i in range(2)]; lzB = [Buf(), Buf()]
        lo = [self.sb(ph, [P, FH], F32, f"lo{i}") for i in range(2)]; loB = [Buf(), Buf()]
        bf, bfb = self.small["fox_bfb"]
        self.dma(wf[:], d["wb_foxin"].rearrange("(kc p) n -> p kc n", p=P)[:, :, 3 * D:3 * D + FH],
                 reads=self.wbufs["foxin"], writes=[wfB], key="c_wf")
        for i in range(2):
            self.op("pool", lambda e, i=i: e.memset(vps[i][:], 1.0), writes=[vpB[i]])
        ntile = NB // 4
        loads = []
        plan = []
        for t in range(ntile):
            needq = t * 4 <= OWN
            for kind, base in (("q", 0), ("k", D), ("v", 2 * D), ("kt", D)):
                if kind == "q" and not needq:
                    continue
                if kind == "kt" and not needq:
                    continue
                for gi in range(D // 512):
                    plan.append((t, kind, gi))
                    loads.append(self.wload_cols("wb_foxin", "foxin", base + gi * 512, 512))
        wp = Builder.WPipe(self, self.wslots, self.wsb, loads)
        pi = 0
        cnt = 0
        for t in range(ntile):
            hs = t % 2
            self.load_xT(xT, xTB, lambda b, t=t: d["xloc"][(t * 4 + b) * P:(t * 4 + b + 1) * P, :], 4, xin, xinB)
            import os
            ksub = int(os.environ.get("KSUB", "9"))
            if ksub < 2:
                continue
            self.norm_mod(xT, xTB, hT[hs], hTB[hs], 512, [(0, 512, 0)],
                          lambda kc, cd: self.modcol(0, "A1", kc, cd), lambda kc, cd: self.modcol(0, "B1", kc, cd),
                          sq, sqB, rs, rsB, xr, xrB)
            for b in range(4 if ksub >= 3 else 0):
                j = t * 4 + b
                s = j % 2
                for kc in range(KC):
                    self.op("pe", lambda e, kc=kc, b=b, hs=hs: e.matmul(out=self.ps[7][:, 0:FH], lhsT=hT[hs][:, kc, b * P:(b + 1) * P],
                                                                        rhs=wf[:, kc, :], start=(kc == 0), stop=(kc == KC - 1)),
                            reads=[hTB[hs], wfB], writes=[self.psb[7]])
                self.op("dve", lambda e, s=s: e.tensor_tensor(out=lz[s][:], in0=self.ps[7][:, 0:FH], in1=bf[:], op=ALU.add),
                        reads=[self.psb[7], bfb], writes=[lzB[s]])
                self.op("act", lambda e, s=s: e.activation(out=lz[s][:], in_=lz[s][:], func=AF.Exp, scale=-1.0), reads=[lzB[s]], writes=[lzB[s]])
                self.op("act", lambda e, s=s: e.activation(out=lz[s][:], in_=lz[s][:], func=AF.Ln, bias=self.one, scale=1.0),
                        reads=[lzB[s], self.cstB], writes=[lzB[s]])
                self.op("dve", lambda e, s=s, j=j: e.tensor_scalar(out=self.LF[:, :, j], in0=lz[s][:], scalar1=-1.0, scalar2=None, op0=ALU.mult),
                        reads=[lzB[s]], writes=[self.LFB])
                if 1 <= j <= OWN:
                    self.op("dve", lambda e, s=s: e.tensor_scalar(out=lo[s][:], in0=lz[s][:], scalar1=-1.0, scalar2=None, op0=ALU.mult),
                            reads=[lzB[s]], writes=[loB[s]])
                    self.dma(d["fl"][(j - 1) * P:j * P, :], lo[s][:], reads=[loB[s]], key=f"lo{s}")
            while ksub >= 4 and pi < len(plan) and plan[pi][0] == t:
                _, kind, gi = plan[pi]
                w, wB = wp.get(pi)
                pi += 1
                wv = w[:, 0:KC * 512].rearrange("p (kc n) -> p kc n", n=512)
                if kind not in os.environ.get("KKIND", "q,k,v,kt").split(","):
                    continue
                for i4 in range(4):
                    pb = 3 + (cnt % 4)
                    ss = cnt % 4
                    cnt += 1
                    if kind in ("q", "k"):
                        head = gi * 4 + i4
                        for kc in range(KC):
                            self.op("pe", lambda e, kc=kc, i4=i4, pb=pb, hs=hs, wv=wv: e.matmul(
                                out=self.ps[pb][:, 0:512], lhsT=wv[:, kc, i4 * P:(i4 + 1) * P], rhs=hT[hs][:, kc, :],
                                start=(kc == 0), stop=(kc == KC - 1)), reads=[wB, hTB[hs]], writes=[self.psb[pb]])
                        if cnt % 2:
                            self.op("act", lambda e, pb=pb, ss=ss: e.activation(out=stg[ss][:], in_=self.ps[pb][:, 0:512], func=AF.Copy),
                                    reads=[self.psb[pb]], writes=[stgB[ss]])
                        else:
                            self.op("dve", lambda e, pb=pb, ss=ss: e.tensor_copy(out=stg[ss][:], in_=self.ps[pb][:, 0:512]),
                                    reads=[self.psb[pb]], writes=[stgB[ss]])
                        if kind == "q":
                            n0 = t * 512
                            n1 = min((OWN + 1) * P, n0 + 512)
                            self.dma(d["QT"][head, :, n0:n1], stg[ss][:, 0:n1 - n0], reads=[stgB[ss]], key=f"stg{ss}")
                        else:
                            self.dma(d["KT"][head, :, t * 512:(t + 1) * 512], stg[ss][:], reads=[stgB[ss]], key=f"stg{ss}")
                    else:
                        b = i4
                        j = t * 4 + b
                        own = 1 <= j <= OWN
                        if kind == "kt" and not own:
                            continue
                        for kc in range(KC):
                            self.op("pe", lambda e, kc=kc, b=b, pb=pb, hs=hs, wv=wv: e.matmul(
                                out=self.ps[pb][:, 0:512], lhsT=hT[hs][:, kc, b * P:(b + 1) * P], rhs=wv[:, kc, :],
                                start=(kc == 0), stop=(kc == KC - 1)), reads=[wB, hTB[hs]], writes=[self.psb[pb]])
                        if own:
                            self.op("act", lambda e, pb=pb, ss=ss: e.activation(out=f32s[ss][:], in_=self.ps[pb][:, 0:512], func=AF.Copy),
                                    reads=[self.psb[pb]], writes=[f32B[ss]])
                            dst = d["fk"] if kind == "kt" else d["fv"]
                            self.dma(dst[(j - 1) * P:j * P, gi * 512:(gi + 1) * 512], f32s[ss][:], reads=[f32B[ss]], key=f"f32s{ss}")
                        if kind == "v":
                            vs = j % 2
                            if own:
                                self.op("dve", lambda e, ss=ss, vs=vs: e.tensor_copy(out=vps[vs][:], in_=f32s[ss][:]),
                                        reads=[f32B[ss]], writes=[vpB[vs]])
                            else:
                                self.op("dve", lambda e, pb=pb, vs=vs: e.tensor_copy(out=vps[vs][:], in_=self.ps[pb][:, 0:512]),
                                        reads=[self.psb[pb]], writes=[vpB[vs]])
                            for hh in range(4):
                                self.dma(d["VP"][gi * 4 + hh, :, j, :], vps[vs][:, hh * 128:(hh + 1) * 128], reads=[vpB[vs]], key=f"vps{vs}")

    def phase2(self, ph):
        c = self.c
        D, KC, NB, OWN, FH = c["D"], c["KC"], c["NB"], c["OWN"], c["FH"]
        d = self.dram
        GL = c["GL"]
        NQ = (OWN + 1) * P
        scale = 1.0 / np.sqrt(128.0)
        C = self.sb(ph, [P, FH, NB], F32, "C"); CB = Buf()
        R = self.sb(ph, [P, FH, NB], F32, "R"); RB = Buf()
        Tb = self.sb(ph, [P, FH, P], F32, "Tb"); TbB = Buf()
        for h in range(FH):
            self.op("pe", lambda e, h=h: e.matmul(out=self.ps[6][0:NB, 0:P], lhsT=self.LF[:, h, :], rhs=self.ones32, start=True, stop=True),
                    reads=[self.LFB, self.cfB], writes=[self.psb[6]])
            self.op("dve", lambda e, h=h: e.tensor_copy(out=Tb[0:NB, h, :], in_=self.ps[6][0:NB, 0:P]), reads=[self.psb[6]], writes=[TbB])
            self.op("pe", lambda e, h=h: e.matmul(out=self.ps[7][:, 0:NB], lhsT=self.Ule, rhs=self.LF[:, h, :], start=True, stop=False),
                    reads=[self.LFB, self.cfB], writes=[self.psb[7]])
            self.op("pe", lambda e, h=h: e.matmul(out=self.ps[7][:, 0:NB], lhsT=Tb[0:NB, h, :], rhs=self.SLp[0:NB, 0:NB], start=False, stop=True),
                    reads=[TbB, self.cfB], writes=[self.psb[7]])
            self.op("dve", lambda e, h=h: e.tensor_copy(out=C[:, h, :], in_=self.ps[7][:, 0:NB]), reads=[self.psb[7]], writes=[CB])
            self.op("pe", lambda e, h=h: e.matmul(out=self.ps[6][:, 0:NB], lhsT=self.E127, rhs=C[:, h, :], start=True, stop=True),
                    reads=[CB, self.cfB], writes=[self.psb[6]])
            self.op("dve", lambda e, h=h: e.tensor_copy(out=R[:, h, :], in_=self.ps[6][:, 0:NB]), reads=[self.psb[6]], writes=[RB])
        bm = self.sb(ph, [P, len(GL), NB], F32, "bm"); bmB = Buf()
        self.dma(bm[:], d["bmb"][:, :, :], writes=[bmB], key="c_bm")
        KTs = [self.sb(ph, [P, NB * P], BF16, f"KT{i}") for i in range(2)]; KTB = [Buf(), Buf()]
        VPs = [self.sb(ph, [P, NB, 132], BF16, f"VP{i}") for i in range(2)]; VPB = [Buf(), Buf()]
        QTs = [self.sb(ph, [P, NQ], BF16, f"QT{i}") for i in range(2)]; QTB = [Buf(), Buf()]
        bias = [self.sb(ph, [P, NB], F32, f"bias{i}") for i in range(2)]; biasB = [Buf(), Buf()]
        PT = [self.sb(ph, [P, 512], BF16, f"PT{i}") for i in range(4)]; PTB = [Buf() for _ in range(4)]
        rd = [self.sb(ph, [P, 1], F32, f"rd{i}") for i in range(2)]; rdB = [Buf(), Buf()]
        On = [self.sb(ph, [P, P], F32, f"On{i}") for i in range(2)]; OnB = [Buf(), Buf()]
        oT = [self.sb(ph, [P, P], BF16, f"oT{i}") for i in range(2)]; oTB = [Buf(), Buf()]
        pcnt = 0
        scnt = 0
        ocnt = 0
        bcnt = 0

        def load_head(h):
            s = h % 2
            half = NB * P // 2
            self.dma(KTs[s][:, 0:half], d["KT"][h, :, 0:half], writes=[KTB[s]], key=f"KT{s}")
            self.dma(KTs[s][:, half:], d["KT"][h, :, half:], writes=[KTB[s]], key=f"KT{s}")
            self.dma(VPs[s][:, 0:NB // 2, 0:128], d["VP"][h, :, 0:NB // 2, :], writes=[VPB[s]], key=f"VP{s}")
            self.dma(VPs[s][:, NB // 2:, 0:128], d["VP"][h, :, NB // 2:, :], writes=[VPB[s]], key=f"VP{s}")
            self.dma(QTs[s][:], d["QT"][h, :, :], writes=[QTB[s]], key=f"QT{s}")

        for i in range(2):
            self.op("pool", lambda e, i=i: e.memset(VPs[i][:], 1.0), writes=[VPB[i]])
        load_head(0)
        for h in range(FH):
            s = h % 2
            if h + 1 < FH:
                load_head(h + 1)
            for gi, (b0, nb) in enumerate(GL):
                bs = bcnt % 2
                bcnt += 1
                jref = b0 + (1 if nb >= 2 else 0)
                self.op("dve", lambda e, bs=bs, gi=gi, h=h: e.tensor_tensor(out=bias[bs][:], in0=bm[:, gi, :], in1=C[:, h, :], op=ALU.subtract),
                        reads=[bmB, CB], writes=[biasB[bs]])
                self.op("dve", lambda e, bs=bs, h=h, jref=jref: e.tensor_scalar(out=bias[bs][:], in0=bias[bs][:], scalar1=R[:, h, jref:jref + 1],
                                                                                scalar2=None, op0=ALU.add),
                        reads=[biasB[bs], RB], writes=[biasB[bs]])
                keys = list(range(0, b0 + nb)) + list(range(OWN + 1, NB))
                for ki, j in enumerate(keys):
                    diag = (b0 <= j < b0 + nb)
                    a0 = (j - b0) if diag else 0
                    q0 = (b0 + a0) * P
                    ncol = (nb - a0) * P
                    sb_ = scnt % 2
                    scnt += 1
                    self.op("pe", lambda e, s=s, j=j, q0=q0, ncol=ncol, sb_=sb_: e.matmul(
                        out=self.ps[sb_][:, 0:ncol], lhsT=KTs[s][:, j * P:(j + 1) * P], rhs=QTs[s][:, q0:q0 + ncol], start=True, stop=True),
                        reads=[KTB[s], QTB[s]], writes=[self.psb[sb_]])
                    pp = pcnt % 4
                    pcnt += 1
                    self.op("act", lambda e, pp=pp, sb_=sb_, ncol=ncol, bs=bs, j=j: e.activation(
                        out=PT[pp][:, 0:ncol], in_=self.ps[sb_][:, 0:ncol], func=AF.Exp, bias=bias[bs][:, j:j + 1], scale=float(scale)),
                        reads=[self.psb[sb_], biasB[bs]], writes=[PTB[pp]])
                    if diag:
                        self.op("pool", lambda e, pp=pp: e.tensor_tensor(out=PT[pp][:, 0:P], in0=PT[pp][:, 0:P], in1=self.tri, op=ALU.mult),
                                reads=[PTB[pp], self.cbB], writes=[PTB[pp]])
                    for a in range(a0, nb):
                        self.op("pe", lambda e, pp=pp, a=a, a0=a0, s=s, j=j, ki=ki, last=(ki == len(keys) - 1): e.matmul(
                            out=self.ps[2 + a][:, 0:129], lhsT=PT[pp][:, (a - a0) * P:(a - a0 + 1) * P], rhs=VPs[s][:, j, 0:129],
                            start=(ki == 0), stop=last), reads=[PTB[pp], VPB[s]], writes=[self.psb[2 + a]])
                for a in range(nb):
                    os_ = ocnt % 2
                    ocnt += 1
                    self.op("dve", lambda e, a=a, os_=os_: e.reciprocal(out=rd[os_][:], in_=self.ps[2 + a][:, 128:129]),
                            reads=[self.psb[2 + a]], writes=[rdB[os_]])
                    self.op("dve", lambda e, a=a, os_=os_: e.tensor_scalar(out=On[os_][:], in0=self.ps[2 + a][:, 0:128], scalar1=rd[os_][:, 0:1],
                                                                          scalar2=None, op0=ALU.mult),
                            reads=[self.psb[2 + a], rdB[os_]], writes=[OnB[os_]])
                    self.op("pe", lambda e, os_=os_: e.transpose(out=self.ps[6 + os_][:, 0:P], in_=On[os_][:], identity=self.ident),
                            reads=[OnB[os_], self.cfB], writes=[self.psb[6 + os_]])
                    self.op("act", lambda e, os_=os_: e.activation(out=oT[os_][:], in_=self.ps[6 + os_][:, 0:P], func=AF.Copy),
                            reads=[self.psb[6 + os_]], writes=[oTB[os_]])
                    blk = b0 + a
                    self.dma(d["OT"][h, :, blk * P:(blk + 1) * P], oT[os_][:], reads=[oTB[os_]], key=f"oT{os_}")

    def phase3(self, ph, prompt):
        c = self.c
        D, KC, NB, OWN, FH, SH, KV, T, HC, GRP = c["D"], c["KC"], c["NB"], c["OWN"], c["FH"], c["SH"], c["KV"], c["T"], c["HC"], c["GRP"]
        d = self.dram
        self.xcnt = 0
        self.tcnt = 0
        NT = 384 if prompt else 2 * T
        N = NT
        KVW = KV * 64
        HH = HC // 2
        L_ = {}
        xin = [self.sb(ph, [P, D], F32, "xin0")]; xinB = [Buf()]
        xin.append(xin[0]); xinB.append(xinB[0])
        xT = self.sb(ph, [P, KC, NT], F32, "xT"); xTB = Buf()
        hT = self.sb(ph, [P, KC, NT], BF16, "hT"); hTB = Buf()
        a2 = self.sb(ph, [P, HH, NT], BF16, "a2"); a2B = Buf()
        sq = [self.sb(ph, [P, NT], BF16, f"sq{i}") for i in range(2)]; sqB = [Buf(), Buf()]
        xr = [self.sb(ph, [P, NT], F32, f"xr{i}") for i in range(2)]; xrB = [Buf(), Buf()]
        rs = self.sb(ph, [P, NT], F32, "rs"); rsB = Buf()
        rl = [self.sb(ph, [P, NT], F32, f"rl{i}") for i in range(2)]; rlB = [Buf(), Buf()]
        qtm = self.sb(ph, [P, c["QKV"]], F32, "qtm"); qtmB = Buf()
        rt = [self.sb(ph, [P, SH, 8], F32, f"rt{i}") for i in range(4)]; rtB = [Buf() for _ in range(4)]
        rc = self.sb(ph, [P, SH, 8], F32, "rc"); rsn = self.sb(ph, [P, SH, 8], F32, "rsn"); rcB = Buf()
        QTa = self.sb(ph, [64, SH, P], BF16, "QTa"); QTaB = Buf()
        NCH = NT // 64 if prompt else 1
        KTa = self.sb(ph, [64, KV, 128 + max(NT, 64)], BF16, "KTa"); KTaB = Buf()
        Vc = self.sb(ph, [64, 2 + NCH, KV, 68], BF16, "Vc"); VcB = Buf()
        hal, halB = self.small["halom"]
        sk_, skB = self.small["sinkb"]
        esink = self.sb(ph, [P, SH], F32, "esink"); esB = Buf()
        self.op("act", lambda e: e.activation(out=esink[:], in_=sk_[:], func=AF.Exp), reads=[skB], writes=[esB])
        PTs = [self.sb(ph, [P, 512], BF16, f"PTs{i}") for i in range(3)]; PTsB = [Buf() for _ in range(3)]
        den = [self.sb(ph, [P, 4], F32, f"den{i}") for i in range(2)]; denB = [Buf(), Buf()]
        Ob = self.sb(ph, [P, max(GRP * 64, P) if prompt else D], F32, "Ob"); ObB = Buf()
        ystg = [self.sb(ph, [P, 512], F32, f"ystg{i}") for i in range(2)]; ystgB = [Buf(), Buf()]
        zb = self.sb(ph, [P, 1], F32, "zb")
        oTt = self.sb(ph, [P, KC, NT], BF16, "oTt"); oTtB = Buf()
        self.op("pool", lambda e: e.memset(KTa[:], 0.0), writes=[KTaB])
        self.op("pool", lambda e: e.memset(Vc[:], 0.0), writes=[VcB])
        self.op("pool", lambda e: e.memset(Vc[:, :, :, 64:65], 1.0), writes=[VcB])
        self.op("pool", lambda e: e.memset(zb[:], 0.0), writes=[esB])
        ycnt = [0]

        ntile = (OWN + 1) // 3 if prompt else 1
        nblk = 3 if prompt else 1
        rows = P if prompt else 2 * T
        segs = [(0, NT, 0)] if prompt else [(0, T, 1), (T, 2 * T, 2)]
        loads = []
        if not prompt:
            for base in (0, D, 2 * D, D):
                for gi in range(D // 512):
                    loads.append(self.wload_cols("wb_foxin", "foxin", base + gi * 512, 512))
        for _ in range(ntile):
            for l in range(2):
                nm = "foxout" if l == 0 else "swaout"
                if l == 1:
                    for b in range(nblk):
                        for gi in range(c["QKV"] // 512):
                            loads.append(self.wload_cols("wb_swain", "swain", gi * 512, 512))
                for gi in range(D // 512):
                    loads.append(self.wload_cols("wb_" + nm, nm, gi * 512, 512))
                ngu = c["DFF"] // 512
                for half in range(2):
                    for gi in range(ngu // 2):
                        loads.append(self.wload_cols("wb_up", f"up{l}", (half * (ngu // 2) + gi) * 512, 512, l))
                    for dc in range(KC):
                        loads.append(self.wload_dn(l, dc, half))
        wp = Builder.WPipe(self, self.wslots, self.wsb, loads)
        self.wi = 0
        mmc = [0]

        def wnext():
            w, wB = wp.get(self.wi)
            self.wi += 1
            return w, wB

        def proj_fm(rhsT, rhsB, nk, ngroups, evac, dn=False):
            for gi in range(ngroups):
                w, wB = wnext()
                if dn:
                    wv = w[:, 0:HH * P].rearrange("p (k n) -> p k n", n=P)
                    chunks = [(gi, lambda kc, wv=wv: wv[:, kc, :])]
                else:
                    wv = w[:, 0:KC * 512].rearrange("p (k n) -> p k n", n=512)
                    chunks = [(gi * 4 + i, (lambda kc, wv=wv, i=i: wv[:, kc, i * P:(i + 1) * P])) for i in range(4)]
                for ch, lf in chunks:
                    pb = 3 + (mmc[0] % 4)
                    mmc[0] += 1
                    for kc in range(nk):
                        self.op("pe", lambda e, kc=kc, pb=pb, lf=lf: e.matmul(out=self.ps[pb][:, 0:N], lhsT=lf(kc), rhs=rhsT[:, kc, 0:N],
                                                                             start=(kc == 0), stop=(kc == nk - 1)),
                                reads=[wB, rhsB], writes=[self.psb[pb]])
                    evac(ch, pb)

        def resid(l, gate):
            def ev(ch, pb):
                for (c0, c1, cd) in segs:
                    self.op("dve", lambda e, ch=ch, pb=pb, c0=c0, c1=c1, cd=cd: e.scalar_tensor_tensor(
                        out=xT[:, ch, c0:c1], in0=self.ps[pb][:, c0:c1], scalar=self.modcol(l, gate, ch, cd), in1=xT[:, ch, c0:c1],
                        op0=ALU.mult, op1=ALU.add), reads=[self.psb[pb], self.modB, xTB], writes=[xTB])
            return ev

        def ffn(l):
            self.norm_mod(xT, xTB, hT, hTB, N, segs, lambda kc, cd: self.modcol(l, "A2", kc, cd), lambda kc, cd: self.modcol(l, "B2", kc, cd),
                          sq, sqB, rs, rsB, xr, xrB)
            for half in range(2):
                def ev_up(ch, pb):
                    s = ch % 2
                    self.op("act", lambda e, pb=pb, s=s: e.activation(out=rl[s][:, 0:N], in_=self.ps[pb][:, 0:N], func=AF.Relu),
                            reads=[self.psb[pb]], writes=[rlB[s]])
                    self.op("pool", lambda e, ch=ch, s=s: e.tensor_tensor(out=a2[:, ch, 0:N], in0=rl[s][:, 0:N], in1=rl[s][:, 0:N], op=ALU.mult),
                            reads=[rlB[s]], writes=[a2B])
                proj_fm(hT, hTB, KC, c["DFF"] // 512 // 2, ev_up)
                proj_fm(a2, a2B, HH, KC, resid(l, "G2"), dn=True)

        L_.update(locals())
        for ti in range(ntile):
            L_["ti"] = ti
            if prompt:
                self.load_xT(xT, xTB, lambda b, ti=ti: d["xloc"][(ti * 3 + b) * P:(ti * 3 + b + 1) * P, :], 3, xin, xinB)
                self.dma(oTt[:, :, :], d["OT"][:, :, ti * NT:(ti + 1) * NT].rearrange("h p n -> p h n"), writes=[oTtB], key="oTt")
            else:
                self.load_xT(xT, xTB, lambda b: d["xs"][:, :], 1, xin, xinB, rows=2 * T)
                import os
                ks4 = int(os.environ.get("KS4", "9"))
                if ks4 == 0:
                    break
                self.fox_sample(L_, ph)
                if ks4 <= 3:
                    break
            proj_fm(oTt, oTtB, KC, D // 512, resid(0, "G1"))
            ffn(0)
            import os
            if (not prompt) and int(os.environ.get("KS4", "9")) == 4:
                break
            self.norm_mod(xT, xTB, hT, hTB, N, segs, lambda kc, cd: self.modcol(1, "A1", kc, cd), lambda kc, cd: self.modcol(1, "B1", kc, cd),
                          sq, sqB, rs, rsB, xr, xrB)
            if prompt and ti > 0:
                self.op("pool", lambda e: e.tensor_copy(out=KTa[:, :, 0:128], in_=KTa[:, :, NT:NT + 128]), reads=[KTaB], writes=[KTaB])
                self.op("pool", lambda e: e.tensor_copy(out=Vc[:, 0:2, :, :], in_=Vc[:, NCH:NCH + 2, :, :]), reads=[VcB], writes=[VcB])
            for b in range(nblk):
                for gi in range(c["QKV"] // 512):
                    w, wB = wnext()
                    wv = w[:, 0:KC * 512].rearrange("p (k n) -> p k n", n=512)
                    pb = 3 + (mmc[0] % 4)
                    mmc[0] += 1
                    for kc in range(KC):
                        self.op("pe", lambda e, kc=kc, pb=pb, wv=wv, b=b: e.matmul(
                            out=self.ps[pb][0:rows, 0:512], lhsT=hT[:, kc, b * P:b * P + rows], rhs=wv[:, kc, :], start=(kc == 0), stop=(kc == KC - 1)),
                            reads=[wB, hTB], writes=[self.psb[pb]])
                    self.op("act", lambda e, pb=pb, gi=gi: e.activation(out=qtm[0:rows, gi * 512:(gi + 1) * 512], in_=self.ps[pb][0:rows, 0:512], func=AF.Copy),
                            reads=[self.psb[pb]], writes=[qtmB])
                if prompt:
                    r0 = (ti * 3 + b) * P
                    self.dma(rc[0:rows], d["ropec"][r0:r0 + rows, :].rearrange("t (h i) -> t h i", i=8), writes=[rcB], key="rc")
                    self.dma(rsn[0:rows], d["ropes"][r0:r0 + rows, :].rearrange("t (h i) -> t h i", i=8), writes=[rcB], key="rc")
                else:
                    self.dma(rc[0:rows], d["ropecs"][:, :].rearrange("t (h i) -> t h i", i=8), writes=[rcB], key="rc")
                    self.dma(rsn[0:rows], d["ropess"][:, :].rearrange("t (h i) -> t h i", i=8), writes=[rcB], key="rc")
                for (c0, nh) in ((0, SH), (D, KV)):
                    v = qtm[0:rows, c0:c0 + nh * 64].rearrange("p (h n) -> p h n", n=64)
                    x1 = v[:, :, 0:8]
                    x2 = v[:, :, 8:16]
                    cc = rc[0:rows, 0:nh, :]
                    sn = rsn[0:rows, 0:nh, :]
                    for k_, (ia, ib) in enumerate(((x1, cc), (x2, sn), (x2, cc), (x1, sn))):
                        self.op("dve", lambda e, k_=k_, ia=ia, ib=ib, nh=nh: e.tensor_tensor(out=rt[k_][0:rows, 0:nh, :], in0=ia, in1=ib, op=ALU.mult),
                                reads=[qtmB, rcB], writes=[rtB[k_]])
                    self.op("dve", lambda e, x1=x1, nh=nh: e.tensor_tensor(out=x1, in0=rt[0][0:rows, 0:nh, :], in1=rt[1][0:rows, 0:nh, :], op=ALU.subtract),
                            reads=[rtB[0], rtB[1]], writes=[qtmB])
                    self.op("dve", lambda e, x2=x2, nh=nh: e.tensor_tensor(out=x2, in0=rt[2][0:rows, 0:nh, :], in1=rt[3][0:rows, 0:nh, :], op=ALU.add),
                            reads=[rtB[2], rtB[3]], writes=[qtmB])
                if prompt and ti == ntile - 1 and b == 2:
                    self.dma(d["sk"][:, :], qtm[:, D:D + KVW], reads=[qtmB], key="kvo")
                    self.dma(d["sv"][:, :], qtm[:, D + KVW:D + 2 * KVW], reads=[qtmB], key="kvo")
                if not prompt:
                    for s_ in range(2):
                        self.dma(d["sks"][s_, c["WB"] - T:c["WB"], :], qtm[s_ * T:(s_ + 1) * T, D:D + KVW], reads=[qtmB], key="kvo")
                        self.dma(d["svs"][s_, c["WB"] - T:c["WB"], :], qtm[s_ * T:(s_ + 1) * T, D + KVW:D + 2 * KVW], reads=[qtmB], key="kvo")
                        self.dma(d["sks"][s_, 0:c["WB"] - T, :], d["csk"][s_, T:c["WB"], :], key="kvo2")
                        self.dma(d["svs"][s_, 0:c["WB"] - T, :], d["csv"][s_, T:c["WB"], :], key="kvo2")
                tcol0 = b * P
                for h0 in range(0, SH + KV, 4):
                    pb = self.tcnt % 2
                    self.tcnt += 1
                    nh = min(4, SH + KV - h0)
                    for hh in range(nh):
                        h = h0 + hh
                        self.op("pe", lambda e, h=h, hh=hh, pb=pb: e.transpose(
                            out=self.ps[pb][0:64, hh * P:hh * P + rows], in_=qtm[0:rows, h * 64:(h + 1) * 64], identity=self.ident[0:rows, 0:rows]),
                            reads=[qtmB, self.cfB], writes=[self.psb[pb]])
                    src = self.ps[pb][0:64, 0:nh * P].rearrange("p (h n) -> p h n", n=P)[:, :, 0:rows]
                    if h0 < SH:
                        self.op("act", lambda e, h0=h0, nh=nh, src=src: e.activation(
                            out=QTa[:, h0:h0 + nh, 0:rows], in_=src, func=AF.Copy), reads=[self.psb[pb]], writes=[QTaB])
                    else:
                        self.op("act", lambda e, nh=nh, src=src, tcol0=tcol0: e.activation(
                            out=KTa[:, 0:nh, 128 + tcol0:128 + tcol0 + rows], in_=src, func=AF.Copy), reads=[self.psb[pb]], writes=[KTaB])
                if prompt:
                    for cc_ in range(2):
                        self.dma(Vc[:, 2 + b * 2 + cc_, :, 0:64], qtm[cc_ * 64:(cc_ + 1) * 64, D + KVW:D + 2 * KVW].rearrange("p (g n) -> p g n", n=64),
                                 reads=[qtmB], writes=[VcB], key="vcd", eng="pool")
                    self.swa_prompt(L_, b)
                else:
                    self.swa_sample(L_)
            if (not prompt) and int(os.environ.get("KS4", "9")) == 5:
                break
            proj_fm(oTt, oTtB, KC, D // 512, resid(1, "G1"))
            ffn(1)
            if (not prompt) and int(os.environ.get("KS4", "9")) == 6:
                break
            gfin, gfinB = self.small["gfinT"]
            self.norm_mod(xT, xTB, None, xTB, N, [(0, N, 0)], lambda kc, cd: gfin[:, kc:kc + 1], lambda kc, cd: self.zero,
                          sq, sqB, rs, rsB, xr, xrB, out_f32=xT)
            for b in range(nblk):
                if prompt and ti == 0 and b == 0:
                    continue
                for k0 in range(0, KC, 4):
                    pb = self.tcnt % 2
                    self.tcnt += 1
                    ys_ = ycnt[0] % 2
                    ycnt[0] += 1
                    for kk in range(4):
                        kc = k0 + kk
                        self.op("pe", lambda e, kc=kc, kk=kk, pb=pb, b=b: e.transpose(
                            out=self.ps[pb][0:rows, kk * P:(kk + 1) * P], in_=xT[:, kc, b * P:b * P + rows], identity=self.ident),
                            reads=[xTB, self.cfB], writes=[self.psb[pb]])
                    self.op("act", lambda e, pb=pb, ys_=ys_: e.activation(out=ystg[ys_][0:rows, :], in_=self.ps[pb][0:rows, 0:512], func=AF.Copy),
                            reads=[self.psb[pb]], writes=[ystgB[ys_]])
                    if prompt:
                        blk = ti * 3 + b - 1
                        self.dma(d["y"][blk * P:(blk + 1) * P, k0 * P:(k0 + 4) * P], ystg[ys_][:, :], reads=[ystgB[ys_]], key=f"ystg{ys_}")
                    else:
                        self.dma(d["ys"][:, k0 * P:(k0 + 4) * P], ystg[ys_][0:rows, :], reads=[ystgB[ys_]], key=f"ystg{ys_}")

    def swa_prompt(self, L_, b):
        c = self.c
        D, KC, SH, KV, GRP = c["D"], c["KC"], c["SH"], c["KV"], c["GRP"]
        QTa, QTaB, KTa, KTaB, Vc, VcB = L_["QTa"], L_["QTaB"], L_["KTa"], L_["KTaB"], L_["Vc"], L_["VcB"]
        PTs, PTsB, den, denB, Ob, ObB, esink, esB = L_["PTs"], L_["PTsB"], L_["den"], L_["denB"], L_["Ob"], L_["ObB"], L_["esink"], L_["esB"]
        oTt, oTtB, hal, halB, ti = L_["oTt"], L_["oTtB"], L_["hal"], L_["halB"], L_["ti"]
        zb = L_["zb"]
        sc = 1.0 / 8.0
        HPM = 2
        nchg = max(1, GRP * 64 // P)
        if not hasattr(self, "pc"):
            self.pc = 0
        for ql in range(2):
            qc = b * 2 + ql
            for g in range(KV):
                for hp in range(0, GRP, HPM):
                    heads = [g * GRP + hp + i for i in range(min(HPM, GRP - hp))]
                    nh = len(heads)
                    pts = []
                    for kci in range(3):
                        kc_ = qc - 2 + kci
                        kcol = 128 + 64 * kc_
                        sb_ = self.pc % 2
                        pp = self.pc % 3
                        self.pc += 1
                        self.op("pe", lambda e, g=g, kcol=kcol, heads=heads, nh=nh, ql=ql, sb_=sb_: e.matmul(
                            out=self.ps[sb_][0:64, 0:nh * 64], lhsT=KTa[:, g, kcol:kcol + 64], rhs=QTa[:, heads[0]:heads[0] + nh, ql * 64:(ql + 1) * 64],
                            start=True, stop=True), reads=[KTaB, QTaB], writes=[self.psb[sb_]])
                        use_hal = (ti == 0 and 0 <= kc_ < 2)
                        bias_ap = hal[0:64, 0:1] if use_hal else zb[0:64, 0:1]
                        self.op("act", lambda e, pp=pp, sb_=sb_, nh=nh, bias_ap=bias_ap: e.activation(
                            out=PTs[pp][0:64, 0:nh * 64], in_=self.ps[sb_][0:64, 0:nh * 64], func=AF.Exp, bias=bias_ap, scale=sc),
                            reads=[self.psb[sb_], halB, esB], writes=[PTsB[pp]])
                        pts.append((pp, 2 + kc_))
                    for i, h in enumerate(heads):
                        for kci, (pp, vslot) in enumerate(pts):
                            self.op("pe", lambda e, pp=pp, vslot=vslot, i=i, g=g, kci=kci: e.matmul(
                                out=self.ps[2][0:64, i * 65:(i + 1) * 65], lhsT=PTs[pp][0:64, i * 64:(i + 1) * 64], rhs=Vc[:, vslot, g, 0:65],
                                start=(kci == 0), stop=(kci == 2)), reads=[PTsB[pp], VcB], writes=[self.psb[2]])
                    ds_ = (self.pc // 3) % 2
                    ov = self.ps[2][0:64, 0:nh * 65].rearrange("p (h n) -> p h n", n=65)
                    self.op("dve", lambda e, ov=ov, nh=nh, ds_=ds_, h0=heads[0]: e.tensor_tensor(
                        out=den[ds_][0:64, 0:nh], in0=ov[:, :, 64], in1=esink[0:64, h0:h0 + nh], op=ALU.add),
                        reads=[self.psb[2], esB], writes=[denB[ds_]])
                    self.op("dve", lambda e, nh=nh, ds_=ds_: e.reciprocal(out=den[ds_][0:64, 0:nh], in_=den[ds_][0:64, 0:nh]),
                            reads=[denB[ds_]], writes=[denB[ds_]])
                    for i, h in enumerate(heads):
                        hl = h - g * GRP
                        self.op("dve", lambda e, i=i, hl=hl, ds_=ds_, ov=ov: e.tensor_scalar(
                            out=Ob[0:64, hl * 64:(hl + 1) * 64], in0=ov[:, i, 0:64], scalar1=den[ds_][0:64, i:i + 1], scalar2=None, op0=ALU.mult),
                            reads=[self.psb[2], denB[ds_]], writes=[ObB])
                pb = 6 + (self.tcnt % 2)
                self.tcnt += 1
                for kk in range(nchg):
                    self.op("pe", lambda e, kk=kk, pb=pb: e.transpose(
                        out=self.ps[pb][:, kk * 64:(kk + 1) * 64], in_=Ob[0:64, kk * P:(kk + 1) * P], identity=self.ident[0:64, 0:64]),
                        reads=[ObB, self.cfB], writes=[self.psb[pb]])
                self.op("act", lambda e, pb=pb, g=g, qc=qc: e.activation(
                    out=oTt[:, g * nchg:(g + 1) * nchg, qc * 64:(qc + 1) * 64], in_=self.ps[pb][:, 0:nchg * 64].rearrange("p (k n) -> p k n", n=64), func=AF.Copy),
                    reads=[self.psb[pb]], writes=[oTtB])

    def swa_sample(self, L_):
        c = self.c
        D, KC, SH, KV, GRP, T, WB = c["D"], c["KC"], c["SH"], c["KV"], c["GRP"], c["T"], c["WB"]
        d = self.dram
        KVW = KV * 64
        QTa, QTaB, KTa, KTaB = L_["QTa"], L_["QTaB"], L_["KTa"], L_["KTaB"]
        PTs, PTsB, den, denB, Ob, ObB, esink, esB = L_["PTs"], L_["PTsB"], L_["den"], L_["denB"], L_["Ob"], L_["ObB"], L_["esink"], L_["esB"]
        oTt, oTtB, qtm, qtmB, zb = L_["oTt"], L_["oTtB"], L_["qtm"], L_["qtmB"], L_["zb"]
        cin, cinB = L_["cin"], L_["cinB"]
        Vc, VcB = L_["Vc"], L_["VcB"]
        sc = 1.0 / 8.0
        for s_ in range(2):
            self.dma(cin[0][0:WB, 0:KVW], d["csk"][s_, :, :], writes=[cinB[0]], key="cin0")
            pb = self.tcnt % 2
            self.tcnt += 1
            for g in range(KV):
                self.op("pe", lambda e, g=g, pb=pb: e.transpose(out=self.ps[pb][0:64, g * P:g * P + WB], in_=cin[0][0:WB, g * 64:(g + 1) * 64],
                                                                  identity=self.ident[0:WB, 0:WB]), reads=[cinB[0], self.cfB], writes=[self.psb[pb]])
            self.op("act", lambda e, pb=pb: e.activation(out=KTa[:, 0:KV, 0:WB], in_=self.ps[pb][0:64, 0:KV * P].rearrange("p (g n) -> p g n", n=P)[:, :, 0:WB],
                                                         func=AF.Copy), reads=[self.psb[pb]], writes=[KTaB])
            for cc_ in range(WB // 64):
                self.dma(Vc[:, cc_, :, 0:64], d["csv"][s_, cc_ * 64:(cc_ + 1) * 64, :].rearrange("p (g n) -> p g n", n=64),
                         writes=[VcB], key="vcd", eng="pool")
            self.dma(Vc[0:T, 2, :, 0:64], qtm[s_ * T:(s_ + 1) * T, D + KVW:D + 2 * KVW].rearrange("p (g n) -> p g n", n=64),
                     reads=[qtmB], writes=[VcB], key="vcd", eng="pool")
            for g in range(KV):
                for hp in range(0, GRP, 4):
                    heads = [g * GRP + hp + i for i in range(min(4, GRP - hp))]
                    nh = len(heads)
                    pts = []
                    srcs = [(0, 64, 0), (64, 64, 1), (128 + s_ * T, T, 2)]
                    for kci, (kcol, nk, vslot) in enumerate(srcs):
                        sb_ = kci % 2
                        pp = kci
                        self.op("pe", lambda e, g=g, kcol=kcol, nk=nk, heads=heads, nh=nh, sb_=sb_, s_=s_: e.matmul(
                            out=self.ps[sb_][0:nk, 0:nh * T], lhsT=KTa[:, g, kcol:kcol + nk], rhs=QTa[:, heads[0]:heads[0] + nh, s_ * T:(s_ + 1) * T],
                            start=True, stop=True), reads=[KTaB, QTaB], writes=[self.psb[sb_]])
                        self.op("act", lambda e, pp=pp, sb_=sb_, nh=nh, nk=nk: e.activation(
                            out=PTs[pp][0:nk, 0:nh * T], in_=self.ps[sb_][0:nk, 0:nh * T], func=AF.Exp, bias=zb[0:nk, 0:1], scale=sc),
                            reads=[self.psb[sb_], esB], writes=[PTsB[pp]])
                        pts.append((pp, nk, vslot))
                    for i, h in enumerate(heads):
                        for kci, (pp, nk, vslot) in enumerate(pts):
                            self.op("pe", lambda e, pp=pp, nk=nk, vslot=vslot, i=i, g=g, kci=kci: e.matmul(
                                out=self.ps[2][0:T, i * 65:(i + 1) * 65], lhsT=PTs[pp][0:nk, i * T:(i + 1) * T], rhs=Vc[0:nk, vslot, g, 0:65],
                                start=(kci == 0), stop=(kci == 2)), reads=[PTsB[pp], VcB], writes=[self.psb[2]])
                    ov = self.ps[2][0:T, 0:nh * 65].rearrange("p (h n) -> p h n", n=65)
                    ds_ = (hp // 4) % 2
                    self.op("dve", lambda e, ov=ov, nh=nh, ds_=ds_, h0=heads[0]: e.tensor_tensor(
                        out=den[ds_][0:T, 0:nh], in0=ov[:, :, 64], in1=esink[0:T, h0:h0 + nh], op=ALU.add),
                        reads=[self.psb[2], esB], writes=[denB[ds_]])
                    self.op("dve", lambda e, nh=nh, ds_=ds_: e.reciprocal(out=den[ds_][0:T, 0:nh], in_=den[ds_][0:T, 0:nh]),
                            reads=[denB[ds_]], writes=[denB[ds_]])
                    for i, h in enumerate(heads):
                        self.op("dve", lambda e, i=i, h=h, ds_=ds_, ov=ov: e.tensor_scalar(
                            out=Ob[0:T, h * 64:(h + 1) * 64], in0=ov[:, i, 0:64], scalar1=den[ds_][0:T, i:i + 1], scalar2=None, op0=ALU.mult),
                            reads=[self.psb[2], denB[ds_]], writes=[ObB])
            for k0 in range(0, KC, 4):
                pb = 6 + (self.tcnt % 2)
                self.tcnt += 1
                for kk in range(4):
                    kc = k0 + kk
                    self.op("pe", lambda e, kc=kc, kk=kk, pb=pb: e.transpose(
                        out=self.ps[pb][:, kk * T:(kk + 1) * T], in_=Ob[0:T, kc * P:(kc + 1) * P], identity=self.ident[0:T, 0:T]),
                        reads=[ObB, self.cfB], writes=[self.psb[pb]])
                self.op("act", lambda e, k0=k0, pb=pb, s_=s_: e.activation(
                    out=oTt[:, k0:k0 + 4, s_ * T:(s_ + 1) * T], in_=self.ps[pb][:, 0:4 * T].rearrange("p (k n) -> p k n", n=T), func=AF.Copy),
                    reads=[self.psb[pb]], writes=[oTtB])

    def fox_sample(self, L_, ph):
        c = self.c
        D, KC, FH, T, PB, PAST = c["D"], c["KC"], c["FH"], c["T"], c["PB"], c["PAST"]
        d = self.dram
        xT, xTB, hT, hTB = L_["xT"], L_["xTB"], L_["hT"], L_["hTB"]
        sq, sqB, rs, rsB, xr, xrB = L_["sq"], L_["sqB"], L_["rs"], L_["rsB"], L_["xr"], L_["xrB"]
        oTt, oTtB, PTs, PTsB, Ob, ObB, den, denB = L_["oTt"], L_["oTtB"], L_["PTs"], L_["PTsB"], L_["Ob"], L_["ObB"], L_["den"], L_["denB"]
        ystg, ystgB, ycnt = L_["ystg"], L_["ystgB"], L_["ycnt"]
        wnext = L_["wnext"]
        mmc = L_["mmc"]
        KcT = self.sb(ph, [P, FH, PAST], BF16, "KcT"); KcTB = Buf()
        Vcs = self.sb(ph, [P, PB, FH, 132], BF16, "Vcs"); VcsB = Buf()
        cin = [L_["xin"][0], L_["xin"][0]]; cinB = [L_["xinB"][0], L_["xinB"][0]]
        L_["cin"] = cin; L_["cinB"] = cinB
        lfc = self.sb(ph, [P, PB, FH], F32, "lfc"); lfcB = Buf()
        bc = self.sb(ph, [P, PB + 1, FH], F32, "bc"); bcB = Buf()
        tmpf = self.sb(ph, [P, PB + 1, FH], F32, "tmpf"); tmpfB = Buf()
        lfn = self.sb(ph, [P, FH], F32, "lfn"); lfnB = Buf()
        lfn2 = self.sb(ph, [P, FH], F32, "lfn2"); lfn2B = Buf()
        QTn = self.sb(ph, [P, FH, 2 * T], BF16, "QTn"); KTn = self.sb(ph, [P, FH, 2 * T], BF16, "KTn"); QKnB = Buf()
        Vn = self.sb(ph, [P, FH, 132], BF16, "Vn"); VnB = Buf()
        Vn1 = self.sb(ph, [P, FH, 132], BF16, "Vn1"); Vn1B = Buf()
        wf = self.sb(ph, [P, KC, FH], BF16, "wfs"); wfB = Buf()
        self.op("pool", lambda e: e.memset(Vcs[:], 1.0), writes=[VcsB])
        self.op("pool", lambda e: e.memset(Vn[:], 1.0), writes=[VnB])
        N = 2 * T
        segs = [(0, T, 1), (T, 2 * T, 2)]
        scale = 1.0 / np.sqrt(128.0)
        bf, bfb = self.small["fox_bfb"]
        self.norm_mod(xT, xTB, hT, hTB, N, segs, lambda kc, cd: self.modcol(0, "A1", kc, cd), lambda kc, cd: self.modcol(0, "B1", kc, cd),
                      sq, sqB, rs, rsB, xr, xrB)
        for kind, dst in (("q", QTn), ("k", KTn)):
            for gi in range(D // 512):
                w, wB = wnext()
                wv = w[:, 0:KC * 512].rearrange("p (k n) -> p k n", n=512)
                for i4 in range(4):
                    pb = 3 + (mmc[0] % 4)
                    mmc[0] += 1
                    for kc in range(KC):
                        self.op("pe", lambda e, kc=kc, i4=i4, pb=pb, wv=wv: e.matmul(out=self.ps[pb][:, 0:N], lhsT=wv[:, kc, i4 * P:(i4 + 1) * P],
                                                                                   rhs=hT[:, kc, 0:N], start=(kc == 0), stop=(kc == KC - 1)),
                                reads=[wB, hTB], writes=[self.psb[pb]])
                    self.op("act", lambda e, pb=pb, dst=dst, hd=gi * 4 + i4: e.activation(out=dst[:, hd, :], in_=self.ps[pb][:, 0:N], func=AF.Copy),
                            reads=[self.psb[pb]], writes=[QKnB])
        for kind in ("v", "kt"):
            for gi in range(D // 512):
                w, wB = wnext()
                wv = w[:, 0:KC * 512].rearrange("p (k n) -> p k n", n=512)
                pb = 3 + (mmc[0] % 4)
                mmc[0] += 1
                ys_ = ycnt[0] % 2
                ycnt[0] += 1
                for kc in range(KC):
                    self.op("pe", lambda e, kc=kc, pb=pb, wv=wv: e.matmul(out=self.ps[pb][0:N, 0:512], lhsT=hT[:, kc, 0:N], rhs=wv[:, kc, :],
                                                                         start=(kc == 0), stop=(kc == KC - 1)), reads=[wB, hTB], writes=[self.psb[pb]])
                self.op("act", lambda e, pb=pb, ys_=ys_: e.activation(out=ystg[ys_][0:N, :], in_=self.ps[pb][0:N, 0:512], func=AF.Copy),
                        reads=[self.psb[pb]], writes=[ystgB[ys_]])
                if kind == "v":
                    self.op("dve", lambda e, ys_=ys_, gi=gi: e.tensor_copy(out=Vn[0:N, gi * 4:(gi + 1) * 4, 0:128],
                                                                        in_=ystg[ys_][0:N, :].rearrange("p (h n) -> p h n", n=128)),
                            reads=[ystgB[ys_]], writes=[VnB])
                self.dma(d["fvs" if kind == "v" else "fks"][:, gi * 512:(gi + 1) * 512], ystg[ys_][0:N, :], reads=[ystgB[ys_]], key=f"ystg{ys_}")
        import os
        ks4 = int(os.environ.get("KS4", "9"))
        if ks4 == 1:
            return
        self.dma(Vn1[0:T, :, :], Vn[T:2 * T, :, :], reads=[VnB], writes=[Vn1B], key="vn1")
        self.dma(wf[:], d["wb_foxin"].rearrange("(kc p) n -> p kc n", p=P)[:, :, 3 * D:3 * D + FH], reads=self.wbufs["foxin"], writes=[wfB], key="c_wf2")
        for kc in range(KC):
            self.op("pe", lambda e, kc=kc: e.matmul(out=self.ps[7][0:N, 0:FH], lhsT=hT[:, kc, 0:N], rhs=wf[:, kc, :], start=(kc == 0), stop=(kc == KC - 1)),
                    reads=[hTB, wfB], writes=[self.psb[7]])
        self.op("dve", lambda e: e.tensor_tensor(out=lfn[0:N, :], in0=self.ps[7][0:N, 0:FH], in1=bf[0:N, :], op=ALU.add), reads=[self.psb[7], bfb], writes=[lfnB])
        self.op("act", lambda e: e.activation(out=lfn[0:N, :], in_=lfn[0:N, :], func=AF.Exp, scale=-1.0), reads=[lfnB], writes=[lfnB])
        self.op("act", lambda e: e.activation(out=lfn[0:N, :], in_=lfn[0:N, :], func=AF.Ln, bias=self.one[0:N, :], scale=1.0), reads=[lfnB, self.cstB], writes=[lfnB])
        self.op("dve", lambda e: e.tensor_scalar(out=lfn[0:N, :], in0=lfn[0:N, :], scalar1=-1.0, scalar2=None, op0=ALU.mult), reads=[lfnB], writes=[lfnB])
        self.dma(d["fls"][:, :], lfn[0:N, :], reads=[lfnB], key="lfn")
        for s_ in range(2):
            Vns = Vn if s_ == 0 else Vn1
            VnsB = VnB if s_ == 0 else Vn1B
            self.dma(lfn2[0:T, :], lfn[s_ * T:(s_ + 1) * T, :], reads=[lfnB], writes=[lfn2B], key="lfn2")
            for jb in range(PB):
                cs = jb % 2
                self.dma(cin[cs][:], d["cfk"][s_, jb * P:(jb + 1) * P, :], writes=[cinB[cs]], key=f"cin{cs}")
                for h0 in range(0, FH, 4):
                    pb = self.tcnt % 2
                    self.tcnt += 1
                    for hh in range(4):
                        self.op("pe", lambda e, h=h0 + hh, hh=hh, pb=pb, cs=cs: e.transpose(out=self.ps[pb][:, hh * P:(hh + 1) * P], in_=cin[cs][:, h * P:(h + 1) * P],
                                                                                         identity=self.ident), reads=[cinB[cs], self.cfB], writes=[self.psb[pb]])
                    self.op("act", lambda e, h0=h0, pb=pb, jb=jb: e.activation(out=KcT[:, h0:h0 + 4, jb * P:(jb + 1) * P],
                                                                              in_=self.ps[pb][:, 0:512].rearrange("p (h n) -> p h n", n=P), func=AF.Copy),
                            reads=[self.psb[pb]], writes=[KcTB])
                self.dma(Vcs[:, jb, :, 0:128], d["cfv"][s_, jb * P:(jb + 1) * P, :].rearrange("p (h n) -> p h n", n=128), writes=[VcsB], key="vcd", eng="pool")
            self.dma(lfc[:], d["cfl"][s_].rearrange("(j p) h -> p j h", p=P), writes=[lfcB], key="lfc")
            lf2 = lfc[:].rearrange("p j h -> p (j h)")
            self.op("pe", lambda e: e.matmul(out=self.ps[6][:, 0:PB * FH], lhsT=self.Lgt, rhs=lf2, start=True, stop=True), reads=[lfcB, self.cfB], writes=[self.psb[6]])
            self.op("pe", lambda e: e.matmul(out=self.ps[7][:, 0:PB * FH], lhsT=self.ones32, rhs=lf2, start=True, stop=True), reads=[lfcB, self.cfB], writes=[self.psb[7]])
            self.op("dve", lambda e: e.tensor_copy(out=tmpf[:, 0:PB, :], in_=self.ps[7][:, 0:PB * FH].rearrange("p (j h) -> p j h", h=FH)), reads=[self.psb[7]], writes=[tmpfB])
            self.op("pe", lambda e: e.matmul(out=self.ps[7][:, 0:FH], lhsT=self.ones32[0:T, :], rhs=lfn2[0:T, :], start=True, stop=True),
                    reads=[lfn2B, self.cfB], writes=[self.psb[7]])
            self.op("dve", lambda e: e.tensor_copy(out=tmpf[:, PB, :], in_=self.ps[7][:, 0:FH]), reads=[self.psb[7]], writes=[tmpfB])
            self.op("pe", lambda e: e.matmul(out=self.ps[7][0:T, FH:2 * FH], lhsT=self.Lgt[0:T, 0:T], rhs=lfn2[0:T, :], start=True, stop=True),
                    reads=[lfn2B, self.cfB], writes=[self.psb[7]])
            self.op("dve", lambda e: e.tensor_copy(out=bc[0:T, PB, :], in_=self.ps[7][0:T, FH:2 * FH]), reads=[self.psb[7]], writes=[bcB])
            for j in range(PB - 1, -1, -1):
                self.op("dve", lambda e, j=j: e.tensor_tensor(out=bc[:, j, :], in0=self.ps[6][:, j * FH:(j + 1) * FH], in1=tmpf[:, PB, :], op=ALU.add),
                        reads=[self.psb[6], tmpfB], writes=[bcB])
                if j > 0:
                    self.op("dve", lambda e, j=j: e.tensor_tensor(out=tmpf[:, PB, :], in0=tmpf[:, PB, :], in1=tmpf[:, j, :], op=ALU.add),
                            reads=[tmpfB], writes=[tmpfB])
            if ks4 == 2:
                continue
            for h in range(FH):
                sb_ = h % 2
                pp = h % 3
                for jb in range(PB):
                    self.op("pe", lambda e, h=h, jb=jb, sb_=sb_, s_=s_: e.matmul(out=self.ps[sb_][:, jb * T:(jb + 1) * T], lhsT=KcT[:, h, jb * P:(jb + 1) * P],
                                                                                rhs=QTn[:, h, s_ * T:(s_ + 1) * T], start=True, stop=True),
                            reads=[KcTB, QKnB], writes=[self.psb[sb_]])
                self.op("pe", lambda e, h=h, sb_=sb_, s_=s_: e.matmul(out=self.ps[sb_][0:T, PB * T:(PB + 1) * T], lhsT=KTn[:, h, s_ * T:(s_ + 1) * T],
                                                                     rhs=QTn[:, h, s_ * T:(s_ + 1) * T], start=True, stop=True),
                        reads=[QKnB], writes=[self.psb[sb_]])
                for jb in range(PB):
                    self.op("act", lambda e, h=h, jb=jb, sb_=sb_, pp=pp: e.activation(out=PTs[pp][:, jb * T:(jb + 1) * T], in_=self.ps[sb_][:, jb * T:(jb + 1) * T],
                                                                                     func=AF.Exp, bias=bc[:, jb, h:h + 1], scale=float(scale)),
                            reads=[self.psb[sb_], bcB], writes=[PTsB[pp]])
                self.op("act", lambda e, h=h, sb_=sb_, pp=pp: e.activation(out=PTs[pp][0:T, PB * T:(PB + 1) * T], in_=self.ps[sb_][0:T, PB * T:(PB + 1) * T],
                                                                          func=AF.Exp, bias=bc[0:T, PB, h:h + 1], scale=float(scale)),
                        reads=[self.psb[sb_], bcB], writes=[PTsB[pp]])
                self.op("pool", lambda e, pp=pp: e.tensor_tensor(out=PTs[pp][0:T, PB * T:(PB + 1) * T], in0=PTs[pp][0:T, PB * T:(PB + 1) * T],
                                                                 in1=self.tri[0:T, 0:T], op=ALU.mult), reads=[PTsB[pp], self.cbB], writes=[PTsB[pp]])
                for jb in range(PB):
                    self.op("pe", lambda e, h=h, jb=jb, pp=pp: e.matmul(out=self.ps[2][0:T, 0:129], lhsT=PTs[pp][:, jb * T:(jb + 1) * T], rhs=Vcs[:, jb, h, 0:129],
                                                                       start=(jb == 0), stop=False), reads=[PTsB[pp], VcsB], writes=[self.psb[2]])
                self.op("pe", lambda e, h=h, pp=pp, Vns=Vns: e.matmul(out=self.ps[2][0:T, 0:129], lhsT=PTs[pp][0:T, PB * T:(PB + 1) * T], rhs=Vns[0:T, h, 0:129],
                                                                     start=False, stop=True), reads=[PTsB[pp], VnsB], writes=[self.psb[2]])
                ds_ = h % 2
                self.op("dve", lambda e, ds_=ds_: e.reciprocal(out=den[ds_][0:T, 0:1], in_=self.ps[2][0:T, 128:129]), reads=[self.psb[2]], writes=[denB[ds_]])
                self.op("dve", lambda e, ds_=ds_, h=h: e.tensor_scalar(out=Ob[0:T, h * P:(h + 1) * P], in0=self.ps[2][0:T, 0:128], scalar1=den[ds_][0:T, 0:1],
                                                                       scalar2=None, op0=ALU.mult), reads=[self.psb[2], denB[ds_]], writes=[ObB])
            for k0 in range(0, KC, 4):
                pb = 6 + (self.tcnt % 2)
                self.tcnt += 1
                for kk in range(4):
                    kc = k0 + kk
                    self.op("pe", lambda e, kc=kc, kk=kk, pb=pb: e.transpose(out=self.ps[pb][:, kk * T:(kk + 1) * T], in_=Ob[0:T, kc * P:(kc + 1) * P],
                                                                           identity=self.ident[0:T, 0:T]), reads=[ObB, self.cfB], writes=[self.psb[pb]])
                self.op("act", lambda e, k0=k0, pb=pb, s_=s_: e.activation(out=oTt[:, k0:k0 + 4, s_ * T:(s_ + 1) * T],
                                                                          in_=self.ps[pb][:, 0:4 * T].rearrange("p (k n) -> p k n", n=T), func=AF.Copy),
                        reads=[self.psb[pb]], writes=[oTtB])


def _fm(v, nchunk):
    return np.ascontiguousarray(v.reshape(nchunk, P).T)


def prepare_inputs(cfg, inp, core):
    c = cfg
    D, S, T, KC, NB, OWN, FH, SH, KV, L = c["D"], c["S"], c["T"], c["KC"], c["NB"], c["OWN"], c["FH"], c["SH"], c["KV"], c["L"]
    b, r = core // 4, core % 4
    f = np.float32
    xp = inp["x_prompt"][b]
    own0 = r * OWN
    order = []
    valid = []
    order.append(own0 - 1 if r > 0 else 0); valid.append(r > 0)
    for j in range(OWN):
        order.append(own0 + j); valid.append(True)
    if r == 0:
        oth = list(range(OWN, OWN + c["NO"]))
        ov = [True] * c["NO"]
    else:
        oth = [j for j in range(NB) if not (own0 - 1 <= j < own0 + OWN)]
        ov = [True] * len(oth)
    order += oth; valid += ov
    assert len(order) == NB
    xloc = np.empty((NB * P, D), f)
    for li, tj in enumerate(order):
        if valid[li]:
            xloc[li * P:(li + 1) * P] = xp[tj * P:(tj + 1) * P]
        else:
            xloc[li * P:(li + 1) * P] = 0.0
    m = {"xloc": xloc}
    m["xs"] = np.ascontiguousarray(inp["x_sample"][2 * core:2 * core + 2].reshape(2 * T, D))
    cv = np.stack([inp["c_prompt"][b], inp["c_sample"][2 * core], inp["c_sample"][2 * core + 1]], 0)
    m["cT"] = np.ascontiguousarray(cv.reshape(3, KC, P).transpose(2, 1, 0))
    m["ada_w"] = inp["ada_w"]
    m["ada_bT"] = np.ascontiguousarray(inp["ada_b"].reshape(L, 6 * KC, P).transpose(2, 0, 1))
    m["gmixT"] = np.ascontiguousarray(inp["norm_mix_g"].reshape(L, KC, P).transpose(2, 0, 1))
    m["gffnT"] = np.ascontiguousarray(inp["norm_ffn_g"].reshape(L, KC, P).transpose(2, 0, 1))
    m["gfinT"] = _fm(inp["final_g"], KC)
    m["fox_w_in"] = inp["fox_w_in"][0]
    m["fox_bfb"] = np.ascontiguousarray(np.broadcast_to(inp["fox_b_f"][0][None, :], (P, FH))).astype(f)
    m["fox_w_out"] = inp["fox_w_out"][0]
    m["swa_w_in"] = inp["swa_w_in"][0]
    m["sinkb"] = np.ascontiguousarray(np.broadcast_to(inp["swa_sinks"][0][None, :], (P, SH))).astype(f)
    m["swa_w_out"] = inp["swa_w_out"][0]
    m["w_up"] = inp["ffn_w_up"]
    m["w_dn"] = inp["ffn_w_down"]
    m["cfk"] = np.ascontiguousarray(inp["cache_fox_k"][0, 2 * core:2 * core + 2].reshape(2, c["PAST"], D))
    m["cfv"] = np.ascontiguousarray(inp["cache_fox_v"][0, 2 * core:2 * core + 2].reshape(2, c["PAST"], D))
    m["cfl"] = np.ascontiguousarray(inp["cache_fox_logf"][0, 2 * core:2 * core + 2])
    m["csk"] = np.ascontiguousarray(inp["cache_swa_k"][0, 2 * core:2 * core + 2].reshape(2, c["WB"], KV * 64))
    m["csv"] = np.ascontiguousarray(inp["cache_swa_v"][0, 2 * core:2 * core + 2].reshape(2, c["WB"], KV * 64))
    cf = np.zeros((P, 6 * P), f)
    ii = np.arange(P)
    cf[:, 0:P] = np.eye(P)
    cf[:, P:2 * P] = (ii[:, None] <= ii[None, :])
    cf[:, 2 * P:3 * P] = (ii[:, None] > ii[None, :])
    cf[:, 3 * P:4 * P] = 1.0
    cf[127, 4 * P:5 * P] = 1.0
    slp = np.zeros((P, P), f)
    for a in range(NB):
        for bb in range(NB):
            if valid[a] and valid[bb] and order[a] < order[bb] and not (r == 0 and a >= OWN + 1) and not (r == 0 and bb >= OWN + 1):
                slp[a, bb] = 1.0
    cf[:, 5 * P:6 * P] = slp
    m["cf"] = cf
    GL = c["GL"]
    bm = np.zeros((P, len(GL), NB), f)
    for gi, (b0, nb) in enumerate(GL):
        for j in range(NB):
            if j <= OWN:
                if j == 0 and gi > 0 and r == 0:
                    bm[:, gi, j] = NEG
            else:
                vis = (r > 0) and (order[j] < own0)
                if not vis:
                    bm[:, gi, j] = NEG
    m["bmb"] = bm
    m["halom"] = np.full((P, 1), NEG if r == 0 else 0.0, f)
    half = 8
    inv = (500000.0 ** (-np.arange(half, dtype=np.float32) * 2.0 / 16.0)).astype(f)
    pos = (own0 * P - P + np.arange((OWN + 1) * P)).astype(f)
    pos = np.maximum(pos, 0.0)
    ang = pos[:, None] * inv[None, :]
    m["ropec"] = np.ascontiguousarray(np.tile(np.cos(ang).astype(f), (1, SH)))
    m["ropes"] = np.ascontiguousarray(np.tile(np.sin(ang).astype(f), (1, SH)))
    poss = (c["PAST"] + np.arange(T)).astype(f)
    angs = np.tile(poss[:, None] * inv[None, :], (2, 1))
    m["ropecs"] = np.ascontiguousarray(np.tile(np.cos(angs).astype(f), (1, SH)))
    m["ropess"] = np.ascontiguousarray(np.tile(np.sin(angs).astype(f), (1, SH)))
    return m


_NC_CACHE = {}


def kernel(**inp):
    inp = {k: np.asarray(v) for k, v in inp.items()}
    B, S, D = inp["x_prompt"].shape
    DB, T, _ = inp["x_sample"].shape
    PAST = inp["cache_fox_k"].shape[2]
    WB = inp["cache_swa_k"].shape[2]
    assert B == 2 and DB == 16
    cfg = make_cfg(D, S, T, PAST, WB)
    key = (D, S, T, PAST, WB)
    if key not in _NC_CACHE:
        _NC_CACHE[key] = Builder(cfg).build()
    nc = _NC_CACHE[key]
    in_maps = [prepare_inputs(cfg, inp, core) for core in range(8)]
    res = run_bass_kernel_spmd(nc, in_maps, core_ids=list(range(8)))
    R = res.results
    OWN, FH, KV = cfg["OWN"], cfg["FH"], cfg["KV"]
    f = np.float32
    y = np.empty((2, S, D), f); fk = np.empty((1, 2, S, FH, 128), f); fv = np.empty_like(fk); fl = np.empty((1, 2, S, FH), f)
    ys = np.empty((16, T, D), f); fks = np.empty((1, 16, T, FH, 128), f); fvs = np.empty_like(fks); fls = np.empty((1, 16, T, FH), f)
    sk = np.empty((1, 2, WB, KV, 64), f); sv = np.empty_like(sk)
    sks = np.empty((1, 16, WB, KV, 64), f); svs = np.empty_like(sks)
    n = OWN * P
    for core in range(8):
        b, r = core // 4, core % 4
        o = R[core]
        y[b, r * n:(r + 1) * n] = o["y"]
        fk[0, b, r * n:(r + 1) * n] = o["fk"].reshape(n, FH, 128)
        fv[0, b, r * n:(r + 1) * n] = o["fv"].reshape(n, FH, 128)
        fl[0, b, r * n:(r + 1) * n] = o["fl"]
        ys[2 * core:2 * core + 2] = o["ys"].reshape(2, T, D)
        fks[0, 2 * core:2 * core + 2] = o["fks"].reshape(2, T, FH, 128)
        fvs[0, 2 * core:2 * core + 2] = o["fvs"].reshape(2, T, FH, 128)
        fls[0, 2 * core:2 * core + 2] = o["fls"].reshape(2, T, FH)
        if r == 3:
            sk[0, b] = o["sk"].reshape(WB, KV, 64)
            sv[0, b] = o["sv"].reshape(WB, KV, 64)
        sks[0, 2 * core:2 * core + 2] = o["sks"].reshape(2, WB, KV, 64)
        svs[0, 2 * core:2 * core + 2] = o["svs"].reshape(2, WB, KV, 64)
    return (y, ys, fk, fv, fl, fks, fvs, fls, sk, sv, sks, svs)
```

```python
import contextlib
import numpy as np
import concourse.bass as bass
import concourse.mybir as mybir
from concourse.bass_utils import run_bass_kernel_spmd

F32 = mybir.dt.float32
BF16 = mybir.dt.bfloat16
AF = mybir.ActivationFunctionType
ALU = mybir.AluOpType
P = 128
NEG = -30000.0


class Buf:
    __slots__ = ("w", "r")

    def __init__(self):
        self.w = None
        self.r = {}


class Trk:
    ENG = ("pe", "dve", "act", "pool", "sp")

    def __init__(self):
        self.ops = {e: [] for e in self.ENG}
        self.seq = {e: 0 for e in self.ENG}
        self.seen = {e: {} for e in self.ENG}
        self.dcnt = {}

    def op(self, eng, fn, reads=(), writes=(), dma=None):
        waits = {}
        seen = self.seen[eng]

        def need(tok):
            if tok is None:
                return
            k, v = tok
            if eng == "pe" and k == "pe":
                return
            if seen.get(k, 0) >= v:
                return
            if waits.get(k, 0) < v:
                waits[k] = v

        for b in reads:
            need(b.w)
        for b in writes:
            need(b.w)
            for k, v in b.r.items():
                need((k, v))
        if dma is not None:
            c = self.dcnt.get(dma, 0)
            if eng == "pool" and c > 0:
                need((dma, c))
            c += 16
            self.dcnt[dma] = c
            tok = (dma, c)
            inc = 16
        else:
            self.seq[eng] += 1
            tok = (eng, self.seq[eng])
            inc = 1
        for k, v in waits.items():
            seen[k] = v
        for b in reads:
            if b.r.get(tok[0], 0) < tok[1]:
                b.r[tok[0]] = tok[1]
        for b in writes:
            b.w = tok
            b.r = {}
        self.ops[eng].append((list(waits.items()), fn, tok[0], inc))

    def barrier(self):
        for e in self.ENG:
            waits = []
            for k in self.ENG:
                if k != e and self.seq[k] > self.seen[e].get(k, 0):
                    waits.append((k, self.seq[k]))
                    self.seen[e][k] = self.seq[k]
            for k, v in self.dcnt.items():
                if k.startswith("cast"):
                    continue
                if v > self.seen[e].get(k, 0):
                    waits.append((k, v))
                    self.seen[e][k] = v
            if waits:
                self.ops[e].append((waits, None, None, 0))

    def emit(self, nc, sems):
        with nc.Block() as block:
            for ename, deco in (("pe", block.tensor), ("dve", block.vector), ("act", block.scalar),
                                ("pool", block.gpsimd), ("sp", block.sync)):
                ops = self.ops[ename]

                def body(e, ops=ops):
                    for waits, fn, sk, inc in ops:
                        for k, v in waits:
                            e.wait_ge(sems[k], v)
                        if fn is not None:
                            fn(e).then_inc(sems[sk], inc)

                deco(body)


def make_cfg(D, S, T, PAST, WB):
    c = dict(D=D, S=S, T=T, PAST=PAST, WB=WB)
    c["KC"] = D // P
    c["NB"] = S // P
    c["OWN"] = c["NB"] // 4
    c["NO"] = c["NB"] - c["OWN"] - 1
    c["FH"] = D // 128
    c["SH"] = D // 64
    c["KV"] = 4
    c["GRP"] = c["SH"] // 4
    c["DFF"] = 4 * D
    c["HC"] = c["DFF"] // P
    c["PB"] = PAST // P
    c["L"] = 2
    c["QKV"] = D + 2 * 4 * 64
    assert (c["OWN"] + 1) % 3 == 0 and c["NB"] % 4 == 0 and c["NB"] <= 128
    gl = [(0, 1)]
    b = 1
    while b <= c["OWN"]:
        nb = min(4, c["OWN"] + 1 - b)
        gl.append((b, nb))
        b += nb
    c["GL"] = gl
    return c


class Builder:
    def __init__(self, cfg):
        self.c = cfg
        self.nc = bass.Bass("TRN2", target_bir_lowering=False)
        self.t = Trk()
        self.es = contextlib.ExitStack()
        self.dram = {}
        self.nid = 0

    def din(self, name, shape, dt=F32):
        self.dram[name] = self.nc.dram_tensor(name, list(shape), dt, kind="ExternalInput").ap()
        return self.dram[name]

    def dout(self, name, shape, dt=F32):
        self.dram[name] = self.nc.dram_tensor(name, list(shape), dt, kind="ExternalOutput").ap()
        return self.dram[name]

    def dscr(self, name, shape, dt=BF16):
        self.dram[name] = self.nc.dram_tensor(name, list(shape), dt, kind="Internal").ap()
        return self.dram[name]

    def sb(self, stack, shape, dt, name=None):
        self.nid += 1
        return stack.enter_context(self.nc.sbuf_tensor(f"{name or 't'}_{self.nid}", list(shape), dt))

    def op(self, eng, fn, reads=(), writes=(), dma=None):
        self.t.op(eng, fn, reads, writes, dma)

    def dma(self, out, in_, reads=(), writes=(), key=None, eng="sp"):
        self.op(eng, lambda e: e.dma_start(out=out, in_=in_), reads, writes, dma=key)

    def declare(self):
        c = self.c
        D, KC, NB, OWN, FH, SH, KV, T, L = c["D"], c["KC"], c["NB"], c["OWN"], c["FH"], c["SH"], c["KV"], c["T"], c["L"]
        di = self.din
        di("xloc", [NB * P, D]); di("xs", [2 * T, D]); di("cT", [P, KC, 3])
        di("ada_w", [L, D, 6 * D]); di("ada_bT", [P, L, 6 * KC])
        di("gmixT", [P, L, KC]); di("gffnT", [P, L, KC]); di("gfinT", [P, KC])
        di("fox_w_in", [D, 3 * D + FH]); di("fox_bfb", [P, FH]); di("fox_w_out", [D, D])
        di("swa_w_in", [D, c["QKV"]]); di("sinkb", [P, SH]); di("swa_w_out", [D, D])
        di("w_up", [L, D, c["DFF"]]); di("w_dn", [L, c["DFF"], D])
        di("cfk", [2, c["PAST"], D]); di("cfv", [2, c["PAST"], D]); di("cfl", [2, c["PAST"], FH])
        di("csk", [2, c["WB"], KV * 64]); di("csv", [2, c["WB"], KV * 64])
        di("cf", [P, 6 * P]); di("bmb", [P, len(c["GL"]), NB]); di("halom", [P, 1])
        di("ropec", [(OWN + 1) * P, SH * 8]); di("ropes", [(OWN + 1) * P, SH * 8])
        di("ropecs", [2 * T, SH * 8]); di("ropess", [2 * T, SH * 8])
        do = self.dout
        do("y", [OWN * P, D]); do("ys", [2 * T, D])
        do("fk", [OWN * P, D]); do("fv", [OWN * P, D]); do("fl", [OWN * P, FH])
        do("fks", [2 * T, D]); do("fvs", [2 * T, D]); do("fls", [2 * T, FH])
        do("sk", [P, KV * 64]); do("sv", [P, KV * 64])
        do("sks", [2, c["WB"], KV * 64]); do("svs", [2, c["WB"], KV * 64])
        ds = self.dscr
        ds("wb_foxin", [3 * D // 512, P, KC, 512]); ds("wb_wf", [P, KC, FH]); ds("wb_foxout", [D // 512, P, KC, 512])
        ds("wb_swain", [c["QKV"] // 512, P, KC, 512]); ds("wb_swaout", [D // 512, P, KC, 512])
        ds("wb_up", [L, c["DFF"] // 512, P, KC, 512]); ds("wb_dn", [L, KC, P, c["HC"], P])
        ds("QT", [FH, P, (OWN + 1) * P]); ds("KT", [FH, P, NB * P]); ds("VP", [FH, P, NB, 128]); ds("OT", [FH, P, (OWN + 1) * P])
        self.wbufs = {}

    def cast_weights(self):
        c = self.c
        d = self.dram
        kk = [0]

        def cast(dst, src, name):
            b = Buf()
            self.wbufs.setdefault(name, []).append(b)
            key = f"cast{kk[0] % 4}"
            kk[0] += 1
            self.dma(dst, src, writes=[b], key=key, eng="pool")

        def cast2d(dname, sname, ncols, name, l=None):
            src = d[sname] if l is None else d[sname][l]
            srcv = src.rearrange("(kc p) n -> p kc n", p=P)
            for g in range(ncols // 512):
                dst = d[dname][g] if l is None else d[dname][l, g]
                cast(dst, srcv[:, :, g * 512:(g + 1) * 512], name)

        cast2d("wb_foxin", "fox_w_in", 3 * c["D"], "foxin")
        cast(d["wb_wf"], d["fox_w_in"].rearrange("(kc p) n -> p kc n", p=P)[:, :, 3 * c["D"]:3 * c["D"] + c["FH"]], "foxin")
        cast2d("wb_foxout", "fox_w_out", c["D"], "foxout")
        for l in range(c["L"]):
            cast2d("wb_up", "w_up", c["DFF"], f"up{l}", l)
            src = d["w_dn"][l].rearrange("(hc p) (dc n) -> dc p hc n", p=P, n=P)
            for dc in range(c["KC"]):
                cast(d["wb_dn"][l, dc], src[dc], f"dn{l}")
            if l == 0:
                cast2d("wb_swain", "swa_w_in", c["QKV"], "swain")
                cast2d("wb_swaout", "swa_w_out", c["D"], "swaout")

    class WPipe:
        def __init__(self, bld, slots, sbufs, loads, depth=3):
            self.b, self.slots, self.sbufs, self.loads, self.depth = bld, slots, sbufs, loads, depth
            self.issued = 0

        def get(self, i):
            while self.issued < len(self.loads) and self.issued < i + self.depth:
                j = self.issued
                s = j % len(self.slots)
                self.loads[j](self.slots[s], self.sbufs[s], f"w{s}")
                self.issued += 1
            s = i % len(self.slots)
            return self.slots[s], self.sbufs[s]

    def wload_cols(self, dname, name, c0, ncols, l=None):
        def f(slot, sbuf, key):
            g = c0 // 512
            src = self.dram[dname][g] if l is None else self.dram[dname][l, g]
            self.dma(slot[:, 0:self.c["KC"] * 512], src.rearrange("p kc n -> p (kc n)"), reads=self.wbufs[name], writes=[sbuf], key=key)
        return f

    def wload_dn(self, l, dc, half):
        def f(slot, sbuf, key):
            HH = self.c["HC"] // 2
            self.dma(slot[:, 0:HH * P], self.dram["wb_dn"][l, dc][:, half * HH:(half + 1) * HH, :].rearrange("p hc n -> p (hc n)"),
                     reads=self.wbufs[f"dn{l}"], writes=[sbuf], key=key)
        return f

    def build(self):
        c = self.c
        nc = self.nc
        self.declare()
        D, KC, NB, OWN, FH, SH, KV, T, L = c["D"], c["KC"], c["NB"], c["OWN"], c["FH"], c["SH"], c["KV"], c["T"], c["L"]
        HC = c["HC"]
        d = self.dram
        with self.es as es:
            self.ps = [es.enter_context(nc.psum_tensor(f"ps{i}", [P, 512], F32)) for i in range(8)]
            self.psb = [Buf() for _ in range(8)]
            g = es
            cf = self.sb(g, [P, 6 * P], F32, "cf"); self.cfB = Buf()
            self.dma(cf[:], d["cf"][:, :], writes=[self.cfB], key="c_cf")
            self.ident = cf[:, 0:P]; self.Ule = cf[:, P:2 * P]; self.Lgt = cf[:, 2 * P:3 * P]
            self.ones32 = cf[:, 3 * P:4 * P]; self.E127 = cf[:, 4 * P:5 * P]; self.SLp = cf[:, 5 * P:6 * P]
            cb = self.sb(g, [P, 3 * P], BF16, "cb"); self.cbB = Buf()
            cst = self.sb(g, [P, 4], F32, "cst"); self.cstB = Buf()
            self.op("dve", lambda e: e.memset(cst[:, 0:1], 1e-6), writes=[self.cstB])
            self.op("dve", lambda e: e.memset(cst[:, 1:2], 1.0), reads=[], writes=[self.cstB])
            self.op("dve", lambda e: e.memset(cst[:, 2:3], 0.0), writes=[self.cstB])
            self.eps = cst[:, 0:1]; self.one = cst[:, 1:2]; self.zero = cst[:, 2:3]
            self.op("dve", lambda e: e.tensor_scalar(out=cb[:, 0:P], in0=self.ones32, scalar1=1.0 / D, scalar2=None, op0=ALU.mult),
                    reads=[self.cfB], writes=[self.cbB])
            self.op("dve", lambda e: e.tensor_copy(out=cb[:, P:2 * P], in_=self.Ule), reads=[self.cfB], writes=[self.cbB])
            self.op("dve", lambda e: e.tensor_copy(out=cb[:, 2 * P:3 * P], in_=self.ones32), reads=[self.cfB], writes=[self.cbB])
            self.onesD = cb[:, 0:P]; self.tri = cb[:, P:2 * P]; self.ones16 = cb[:, 2 * P:3 * P]
            small = {}
            for nm, shp in (("ada_bT", [P, L, 6 * KC]), ("gmixT", [P, L, KC]), ("gffnT", [P, L, KC]), ("gfinT", [P, KC]),
                            ("fox_bfb", [P, FH]), ("sinkb", [P, SH]), ("halom", [P, 1]), ("cT", [P, KC, 3])):
                tl = self.sb(g, shp, F32, nm)
                bb = Buf()
                self.dma(tl[:], d[nm][:], writes=[bb], key="c_" + nm)
                small[nm] = (tl, bb)
            self.small = small
            self.modt = self.sb(g, [P, L, 6 * KC, 3], F32, "mod"); self.modB = Buf()
            self.A1 = self.sb(g, [P, L, KC, 3], F32, "A1"); self.A2 = self.sb(g, [P, L, KC, 3], F32, "A2")
            self.wsb = [Buf() for _ in range(3)]
            import os
            stop = int(os.environ.get("KSTOP", "9"))
            if stop >= 1:
                self.cast_weights()
            if stop >= 2:
                self.phase_ada()
            self.t.barrier()
            with contextlib.ExitStack() as ph12:
                self.LF = self.sb(ph12, [P, FH, NB], F32, "LF"); self.LFB = Buf()
                with contextlib.ExitStack() as ph:
                    self.wslots = [self.sb(ph, [P, KC * 512], BF16, f"w{i}") for i in range(3)]
                    if stop >= 3:
                        self.phase1(ph)
                    self.t.barrier()
                with contextlib.ExitStack() as ph:
                    if stop >= 4:
                        self.phase2(ph)
                    self.t.barrier()
            with contextlib.ExitStack() as ph:
                self.wslots = [self.sb(ph, [P, KC * 512], BF16, f"w{i}") for i in range(3)]
                if stop >= 5:
                    self.phase3(ph, True)
                self.t.barrier()
            with contextlib.ExitStack() as ph:
                self.wslots = [self.sb(ph, [P, KC * 512], BF16, f"w{i}") for i in range(3)]
                if stop >= 6:
                    self.phase3(ph, False)
                self.t.barrier()
            keys = list(Trk.ENG) + list(self.t.dcnt.keys())
            with nc.cleanup_on_exit():
                sems = {k: nc.alloc_semaphore(f"s_{k}") for k in keys}
                for k in keys:
                    nc.gpsimd.sem_clear(sems[k])
                nc.all_engine_barrier()
                self.t.emit(nc, sems)
        return nc

    def phase_ada(self):
        c = self.c
        nc = self.nc
        D, KC, L = c["D"], c["KC"], c["L"]
        d = self.dram
        cT, cTb = self.small["cT"]
        with contextlib.ExitStack() as ph:
            sc = self.sb(ph, [P, KC, 3], F32, "silu"); scB = Buf()
            sg = self.sb(ph, [P, KC, 3], F32, "sig")
            self.op("act", lambda e: e.activation(out=sg[:], in_=cT[:], func=AF.Sigmoid), reads=[cTb], writes=[scB])
            self.op("dve", lambda e: e.tensor_tensor(out=sc[:], in0=sg[:], in1=cT[:], op=ALU.mult), reads=[cTb, scB], writes=[scB])
            wa = [self.sb(ph, [P, KC, 512], F32, f"wa{i}") for i in range(2)]
            waB = [Buf(), Buf()]
            ng = 6 * D // 512
            bT, bTb = self.small["ada_bT"]
            pb = 7
            for l in range(L):
                for gi in range(ng):
                    s = (l * ng + gi) % 2
                    self.dma(wa[s][:], d["ada_w"][l].rearrange("(kc p) n -> p kc n", p=P)[:, :, gi * 512:(gi + 1) * 512],
                             writes=[waB[s]], key=f"wa{s}")
                    for j in range(4):
                        ch = gi * 4 + j
                        for kc in range(KC):
                            self.op("pe", lambda e, s=s, j=j, kc=kc, ch=ch: e.matmul(
                                out=self.ps[pb][:, ch * 3:ch * 3 + 3], lhsT=wa[s][:, kc, j * P:(j + 1) * P], rhs=sc[:, kc, :],
                                start=(kc == 0), stop=(kc == KC - 1)), reads=[waB[s], scB], writes=[self.psb[pb]])
                for cd in range(3):
                    self.op("dve", lambda e, l=l, cd=cd: e.tensor_tensor(
                        out=self.modt[:, l, :, cd], in0=self.ps[pb][:, 0:6 * KC * 3].rearrange("p (j c) -> p j c", c=3)[:, :, cd],
                        in1=bT[:, l, :], op=ALU.add), reads=[self.psb[pb], bTb], writes=[self.modB])
            gm, gmb = self.small["gmixT"]
            gf, gfb = self.small["gffnT"]
            for l in range(L):
                for cd in range(3):
                    self.op("dve", lambda e, l=l, cd=cd: e.scalar_tensor_tensor(
                        out=self.A1[:, l, :, cd], in0=self.modt[:, l, KC:2 * KC, cd], scalar=1.0, in1=gm[:, l, :],
                        op0=ALU.add, op1=ALU.mult), reads=[self.modB, gmb], writes=[self.modB])
                    self.op("dve", lambda e, l=l, cd=cd: e.scalar_tensor_tensor(
                        out=self.A2[:, l, :, cd], in0=self.modt[:, l, 4 * KC:5 * KC, cd], scalar=1.0, in1=gf[:, l, :],
                        op0=ALU.add, op1=ALU.mult), reads=[self.modB, gfb], writes=[self.modB])

    def modcol(self, l, kind, kc, cd):
        KC = self.c["KC"]
        if kind == "A1":
            return self.A1[:, l, kc, cd:cd + 1]
        if kind == "A2":
            return self.A2[:, l, kc, cd:cd + 1]
        off = {"B1": 0, "G1": 2, "B2": 3, "G2": 5}[kind]
        return self.modt[:, l, off * KC + kc, cd:cd + 1]

    def load_xT(self, xT, xTB, src_rows_fn, nblk, xin, xinB, rows=P):
        KC = self.c["KC"]
        for b in range(nblk):
            s = self.xcnt % 2
            self.xcnt += 1
            self.dma(xin[s][0:rows, :], src_rows_fn(b), writes=[xinB[s]], key=f"xin{s}")
            for k0 in range(0, KC, 4):
                pb = self.tcnt % 2
                self.tcnt += 1
                nk = min(4, KC - k0)
                for kk in range(nk):
                    kc = k0 + kk
                    self.op("pe", lambda e, s=s, kc=kc, kk=kk, pb=pb: e.transpose(
                        out=self.ps[pb][:, kk * P:kk * P + rows], in_=xin[s][0:rows, kc * P:(kc + 1) * P], identity=self.ident[0:rows, 0:rows]),
                        reads=[xinB[s], self.cfB], writes=[self.psb[pb]])
                eng = "act" if (self.tcnt % 2) else "dve"
                if eng == "act":
                    self.op("act", lambda e, k0=k0, nk=nk, pb=pb, b=b: e.activation(
                        out=xT[:, k0:k0 + nk, b * rows:(b + 1) * rows], in_=self.ps[pb][:, 0:nk * P].rearrange("p (k n) -> p k n", n=P)[:, :, 0:rows],
                        func=AF.Copy), reads=[self.psb[pb]], writes=[xTB])
                else:
                    self.op("dve", lambda e, k0=k0, nk=nk, pb=pb, b=b: e.tensor_copy(
                        out=xT[:, k0:k0 + nk, b * rows:(b + 1) * rows], in_=self.ps[pb][:, 0:nk * P].rearrange("p (k n) -> p k n", n=P)[:, :, 0:rows]),
                        reads=[self.psb[pb]], writes=[xTB])

    def norm_mod(self, xT, xTB, hT, hTB, N, segs, Afn, Bfn, sq, sqB, rs, rsB, xr, xrB, out_f32=None):
        KC = self.c["KC"]
        pb = 2
        for kc in range(KC):
            s = kc % 2
            self.op("pool", lambda e, kc=kc, s=s: e.tensor_tensor(out=sq[s][:, 0:N], in0=xT[:, kc, 0:N], in1=xT[:, kc, 0:N], op=ALU.mult),
                    reads=[xTB], writes=[sqB[s]])
            self.op("pe", lambda e, kc=kc, s=s: e.matmul(out=self.ps[pb][:, 0:N], lhsT=self.onesD, rhs=sq[s][:, 0:N],
                                                        start=(kc == 0), stop=(kc == KC - 1)),
                    reads=[sqB[s], self.cbB], writes=[self.psb[pb]])
        self.op("act", lambda e: e.activation(out=rs[:, 0:N], in_=self.ps[pb][:, 0:N], func=AF.Sqrt, bias=self.eps, scale=1.0),
                reads=[self.psb[pb], self.cstB], writes=[rsB])
        self.op("dve", lambda e: e.reciprocal(out=rs[:, 0:N], in_=rs[:, 0:N]), reads=[rsB], writes=[rsB])
        for kc in range(KC):
            s = kc % 2
            self.op("dve", lambda e, kc=kc, s=s: e.tensor_tensor(out=xr[s][:, 0:N], in0=xT[:, kc, 0:N], in1=rs[:, 0:N], op=ALU.mult),
                    reads=[xTB, rsB], writes=[xrB[s]])
            for (c0, c1, cd) in segs:
                dst = hT if out_f32 is None else out_f32
                self.op("act", lambda e, kc=kc, s=s, c0=c0, c1=c1, cd=cd, dst=dst: e.activation(
                    out=dst[:, kc, c0:c1], in_=xr[s][:, c0:c1], func=AF.Identity, bias=Bfn(kc, cd), scale=Afn(kc, cd)),
                    reads=[xrB[s], self.modB], writes=[hTB])

    def phase1(self, ph):
        c = self.c
        D, KC, NB, OWN, FH = c["D"], c["KC"], c["NB"], c["OWN"], c["FH"]
        d = self.dram
        self.xcnt = 0
        self.tcnt = 0
        xin = [self.sb(ph, [P, D], F32, f"xin{i}") for i in range(2)]; xinB = [Buf(), Buf()]
        xT = self.sb(ph, [P, KC, 512], F32, "xT"); xTB = Buf()
        hT = [self.sb(ph, [P, KC, 512], BF16, f"hT{i}") for i in range(2)]; hTB = [Buf(), Buf()]
        sq = [self.sb(ph, [P, 512], BF16, f"sq{i}") for i in range(2)]; sqB = [Buf(), Buf()]
        xr = [self.sb(ph, [P, 512], F32, f"xr{i}") for i in range(2)]; xrB = [Buf(), Buf()]
        rs = self.sb(ph, [P, 512], F32, "rs"); rsB = Buf()
        stg = [self.sb(ph, [P, 512], BF16, f"stg{i}") for i in range(4)]; stgB = [Buf() for _ in range(4)]
        f32s = [self.sb(ph, [P, 512], F32, f"f32s{i}") for i in range(4)]; f32B = [Buf() for _ in range(4)]
        vps = [self.sb(ph, [P, 512], BF16, f"vps{i}") for i in range(2)]; vpB = [Buf(), Buf()]
        wf = self.sb(ph, [P, KC, FH], BF16, "wf"); wfB = Buf()
        lz = [self.sb(ph, [P, FH], F32, f"lz{i}") for i in range(2)]; lzB = [Buf(), Buf()]
        lo = [self.sb(ph, [P, FH], F32, f"lo{i}") for i in range(2)]; loB = [Buf(), Buf()]
        bf, bfb = self.small["fox_bfb"]
        self.dma(wf[:], d["wb_wf"][:, :, :], reads=self.wbufs["foxin"], writes=[wfB], key="c_wf")
        for i in range(2):
            self.op("pool", lambda e, i=i: e.memset(vps[i][:], 1.0), writes=[vpB[i]])
        ntile = NB // 4
        loads = []
        plan = []
        for t in range(ntile):
            needq = t * 4 <= OWN
            for kind, base in (("q", 0), ("k", D), ("v", 2 * D), ("kt", D)):
                if kind == "q" and not needq:
                    continue
                if kind == "kt" and not needq:
                    continue
                for gi in range(D // 512):
                    plan.append((t, kind, gi))
                    loads.append(self.wload_cols("wb_foxin", "foxin", base + gi * 512, 512))
        wp = Builder.WPipe(self, self.wslots, self.wsb, loads)
        pi = 0
        cnt = 0
        for t in range(ntile):
            hs = t % 2
            self.load_xT(xT, xTB, lambda b, t=t: d["xloc"][(t * 4 + b) * P:(t * 4 + b + 1) * P, :], 4, xin, xinB)
            import os
            ksub = int(os.environ.get("KSUB", "9"))
            if ksub < 2:
                continue
            self.norm_mod(xT, xTB, hT[hs], hTB[hs], 512, [(0, 512, 0)],
                          lambda kc, cd: self.modcol(0, "A1", kc, cd), lambda kc, cd: self.modcol(0, "B1", kc, cd),
                          sq, sqB, rs, rsB, xr, xrB)
            for b in range(4 if ksub >= 3 else 0):
                j = t * 4 + b
                s = j % 2
                for kc in range(KC):
                    self.op("pe", lambda e, kc=kc, b=b, hs=hs: e.matmul(out=self.ps[7][:, 0:FH], lhsT=hT[hs][:, kc, b * P:(b + 1) * P],
                                                                        rhs=wf[:, kc, :], start=(kc == 0), stop=(kc == KC - 1)),
                            reads=[hTB[hs], wfB], writes=[self.psb[7]])
                self.op("dve", lambda e, s=s: e.tensor_tensor(out=lz[s][:], in0=self.ps[7][:, 0:FH], in1=bf[:], op=ALU.add),
                        reads=[self.psb[7], bfb], writes=[lzB[s]])
                self.op("act", lambda e, s=s: e.activation(out=lz[s][:], in_=lz[s][:], func=AF.Exp, scale=-1.0), reads=[lzB[s]], writes=[lzB[s]])
                self.op("act", lambda e, s=s: e.activation(out=lz[s][:], in_=lz[s][:], func=AF.Ln, bias=self.one, scale=1.0),
                        reads=[lzB[s], self.cstB], writes=[lzB[s]])
                self.op("dve", lambda e, s=s, j=j: e.tensor_scalar(out=self.LF[:, :, j], in0=lz[s][:], scalar1=-1.0, scalar2=None, op0=ALU.mult),
                        reads=[lzB[s]], writes=[self.LFB])
                if 1 <= j <= OWN:
                    self.op("dve", lambda e, s=s: e.tensor_scalar(out=lo[s][:], in0=lz[s][:], scalar1=-1.0, scalar2=None, op0=ALU.mult),
                            reads=[lzB[s]], writes=[loB[s]])
                    self.dma(d["fl"][(j - 1) * P:j * P, :], lo[s][:], reads=[loB[s]], key=f"lo{s}")
            while ksub >= 4 and pi < len(plan) and plan[pi][0] == t:
                _, kind, gi = plan[pi]
                w, wB = wp.get(pi)
                pi += 1
                wv = w[:, 0:KC * 512].rearrange("p (kc n) -> p kc n", n=512)
                if kind not in os.environ.get("KKIND", "q,k,v,kt").split(","):
                    continue
                for i4 in range(4):
                    pb = 3 + (cnt % 4)
                    ss = cnt % 4
                    cnt += 1
                    if kind in ("q", "k"):
                        head = gi * 4 + i4
                        for kc in range(KC):
                            self.op("pe", lambda e, kc=kc, i4=i4, pb=pb, hs=hs, wv=wv: e.matmul(
                                out=self.ps[pb][:, 0:512], lhsT=wv[:, kc, i4 * P:(i4 + 1) * P], rhs=hT[hs][:, kc, :],
                                start=(kc == 0), stop=(kc == KC - 1)), reads=[wB, hTB[hs]], writes=[self.psb[pb]])
                        if cnt % 2:
                            self.op("act", lambda e, pb=pb, ss=ss: e.activation(out=stg[ss][:], in_=self.ps[pb][:, 0:512], func=AF.Copy),
                                    reads=[self.psb[pb]], writes=[stgB[ss]])
                        else:
                            self.op("dve", lambda e, pb=pb, ss=ss: e.tensor_copy(out=stg[ss][:], in_=self.ps[pb][:, 0:512]),
                                    reads=[self.psb[pb]], writes=[stgB[ss]])
                        if kind == "q":
                            n0 = t * 512
                            n1 = min((OWN + 1) * P, n0 + 512)
                            self.dma(d["QT"][head, :, n0:n1], stg[ss][:, 0:n1 - n0], reads=[stgB[ss]], key=f"stg{ss}")
                        else:
                            self.dma(d["KT"][head, :, t * 512:(t + 1) * 512], stg[ss][:], reads=[stgB[ss]], key=f"stg{ss}")
                    else:
                        b = i4
                        j = t * 4 + b
                        own = 1 <= j <= OWN
                        if kind == "kt" and not own:
                            continue
                        for kc in range(KC):
                            self.op("pe", lambda e, kc=kc, b=b, pb=pb, hs=hs, wv=wv: e.matmul(
                                out=self.ps[pb][:, 0:512], lhsT=hT[hs][:, kc, b * P:(b + 1) * P], rhs=wv[:, kc, :],
                                start=(kc == 0), stop=(kc == KC - 1)), reads=[wB, hTB[hs]], writes=[self.psb[pb]])
                        if own:
                            self.op("act", lambda e, pb=pb, ss=ss: e.activation(out=f32s[ss][:], in_=self.ps[pb][:, 0:512], func=AF.Copy),
                                    reads=[self.psb[pb]], writes=[f32B[ss]])
                            dst = d["fk"] if kind == "kt" else d["fv"]
                            self.dma(dst[(j - 1) * P:j * P, gi * 512:(gi + 1) * 512], f32s[ss][:], reads=[f32B[ss]], key=f"f32s{ss}")
                        if kind == "v":
                            vs = j % 2
                            if own:
                                self.op("dve", lambda e, ss=ss, vs=vs: e.tensor_copy(out=vps[vs][:], in_=f32s[ss][:]),
                                        reads=[f32B[ss]], writes=[vpB[vs]])
                            else:
                                self.op("dve", lambda e, pb=pb, vs=vs: e.tensor_copy(out=vps[vs][:], in_=self.ps[pb][:, 0:512]),
                                        reads=[self.psb[pb]], writes=[vpB[vs]])
                            for hh in range(4):
                                self.dma(d["VP"][gi * 4 + hh, :, j, :], vps[vs][:, hh * 128:(hh + 1) * 128], reads=[vpB[vs]], key=f"vps{vs}")

    def phase2(self, ph):
        c = self.c
        D, KC, NB, OWN, FH = c["D"], c["KC"], c["NB"], c["OWN"], c["FH"]
        d = self.dram
        GL = c["GL"]
        NQ = (OWN + 1) * P
        scale = 1.0 / np.sqrt(128.0)
        C = self.sb(ph, [P, FH, NB], F32, "C"); CB = Buf()
        R = self.sb(ph, [P, FH, NB], F32, "R"); RB = Buf()
        Tb = self.sb(ph, [P, FH, P], F32, "Tb"); TbB = Buf()
        for h in range(FH):
            self.op("pe", lambda e, h=h: e.matmul(out=self.ps[6][0:NB, 0:P], lhsT=self.LF[:, h, :], rhs=self.ones32, start=True, stop=True),
                    reads=[self.LFB, self.cfB], writes=[self.psb[6]])
            self.op("dve", lambda e, h=h: e.tensor_copy(out=Tb[0:NB, h, :], in_=self.ps[6][0:NB, 0:P]), reads=[self.psb[6]], writes=[TbB])
            self.op("pe", lambda e, h=h: e.matmul(out=self.ps[7][:, 0:NB], lhsT=self.Ule, rhs=self.LF[:, h, :], start=True, stop=False),
                    reads=[self.LFB, self.cfB], writes=[self.psb[7]])
            self.op("pe", lambda e, h=h: e.matmul(out=self.ps[7][:, 0:NB], lhsT=Tb[0:NB, h, :], rhs=self.SLp[0:NB, 0:NB], start=False, stop=True),
                    reads=[TbB, self.cfB], writes=[self.psb[7]])
            self.op("dve", lambda e, h=h: e.tensor_copy(out=C[:, h, :], in_=self.ps[7][:, 0:NB]), reads=[self.psb[7]], writes=[CB])
            self.op("pe", lambda e, h=h: e.matmul(out=self.ps[6][:, 0:NB], lhsT=self.E127, rhs=C[:, h, :], start=True, stop=True),
                    reads=[CB, self.cfB], writes=[self.psb[6]])
            self.op("dve", lambda e, h=h: e.tensor_copy(out=R[:, h, :], in_=self.ps[6][:, 0:NB]), reads=[self.psb[6]], writes=[RB])
        bm = self.sb(ph, [P, len(GL), NB], F32, "bm"); bmB = Buf()
        self.dma(bm[:], d["bmb"][:, :, :], writes=[bmB], key="c_bm")
        KTs = [self.sb(ph, [P, NB * P], BF16, f"KT{i}") for i in range(2)]; KTB = [Buf(), Buf()]
        VPs = [self.sb(ph, [P, NB, 132], BF16, f"VP{i}") for i in range(2)]; VPB = [Buf(), Buf()]
        QTs = [self.sb(ph, [P, NQ], BF16, f"QT{i}") for i in range(2)]; QTB = [Buf(), Buf()]
        bias = [self.sb(ph, [P, NB], F32, f"bias{i}") for i in range(2)]; biasB = [Buf(), Buf()]
        PT = [self.sb(ph, [P, 512], BF16, f"PT{i}") for i in range(4)]; PTB = [Buf() for _ in range(4)]
        rd = [self.sb(ph, [P, 1], F32, f"rd{i}") for i in range(2)]; rdB = [Buf(), Buf()]
        On = [self.sb(ph, [P, P], F32, f"On{i}") for i in range(2)]; OnB = [Buf(), Buf()]
        oT = [self.sb(ph, [P, P], BF16, f"oT{i}") for i in range(2)]; oTB = [Buf(), Buf()]
        pcnt = 0
        scnt = 0
        ocnt = 0
        bcnt = 0

        def load_head(h):
            s = h % 2
            half = NB * P // 2
            self.dma(KTs[s][:, 0:half], d["KT"][h, :, 0:half], writes=[KTB[s]], key=f"KT{s}")
            self.dma(KTs[s][:, half:], d["KT"][h, :, half:], writes=[KTB[s]], key=f"KT{s}")
            self.dma(VPs[s][:, 0:NB // 2, 0:128], d["VP"][h, :, 0:NB // 2, :], writes=[VPB[s]], key=f"VP{s}")
            self.dma(VPs[s][:, NB // 2:, 0:128], d["VP"][h, :, NB // 2:, :], writes=[VPB[s]], key=f"VP{s}")
            self.dma(QTs[s][:], d["QT"][h, :, :], writes=[QTB[s]], key=f"QT{s}")

        for i in range(2):
            self.op("pool", lambda e, i=i: e.memset(VPs[i][:], 1.0), writes=[VPB[i]])
        load_head(0)
        for h in range(FH):
            s = h % 2
            if h + 1 < FH:
                load_head(h + 1)
            for gi, (b0, nb) in enumerate(GL):
                bs = bcnt % 2
                bcnt += 1
                jref = b0 + (1 if nb >= 2 else 0)
                self.op("dve", lambda e, bs=bs, gi=gi, h=h: e.tensor_tensor(out=bias[bs][:], in0=bm[:, gi, :], in1=C[:, h, :], op=ALU.subtract),
                        reads=[bmB, CB], writes=[biasB[bs]])
                self.op("dve", lambda e, bs=bs, h=h, jref=jref: e.tensor_scalar(out=bias[bs][:], in0=bias[bs][:], scalar1=R[:, h, jref:jref + 1],
                                                                                scalar2=None, op0=ALU.add),
                        reads=[biasB[bs], RB], writes=[biasB[bs]])
                keys = list(range(0, b0 + nb)) + list(range(OWN + 1, NB))
                def geom(j):
                    diag = (b0 <= j < b0 + nb)
                    a0 = (j - b0) if diag else 0
                    return diag, a0, (b0 + a0) * P, (nb - a0) * P

                def issue_S(ki):
                    j = keys[ki]
                    diag, a0, q0, ncol = geom(j)
                    sb_ = ki % 2
                    self.op("pe", lambda e, s=s, j=j, q0=q0, ncol=ncol, sb_=sb_: e.matmul(
                        out=self.ps[sb_][:, 0:ncol], lhsT=KTs[s][:, j * P:(j + 1) * P], rhs=QTs[s][:, q0:q0 + ncol], start=True, stop=True),
                        reads=[KTB[s], QTB[s]], writes=[self.psb[sb_]])

                issue_S(0)
                for ki, j in enumerate(keys):
                    diag, a0, q0, ncol = geom(j)
                    sb_ = ki % 2
                    if ki + 1 < len(keys):
                        issue_S(ki + 1)
                    pp = pcnt % 4
                    pcnt += 1
                    self.op("act", lambda e, pp=pp, sb_=sb_, ncol=ncol, bs=bs, j=j: e.activation(
                        out=PT[pp][:, 0:ncol], in_=self.ps[sb_][:, 0:ncol], func=AF.Exp, bias=bias[bs][:, j:j + 1], scale=float(scale)),
                        reads=[self.psb[sb_], biasB[bs]], writes=[PTB[pp]])
                    if diag:
                        self.op("pool", lambda e, pp=pp: e.tensor_tensor(out=PT[pp][:, 0:P], in0=PT[pp][:, 0:P], in1=self.tri, op=ALU.mult),
                                reads=[PTB[pp], self.cbB], writes=[PTB[pp]])
                    for a in range(a0, nb):
                        self.op("pe", lambda e, pp=pp, a=a, a0=a0, s=s, j=j, ki=ki, last=(ki == len(keys) - 1): e.matmul(
                            out=self.ps[2 + a][:, 0:129], lhsT=PT[pp][:, (a - a0) * P:(a - a0 + 1) * P], rhs=VPs[s][:, j, 0:129],
                            start=(ki == 0), stop=last), reads=[PTB[pp], VPB[s]], writes=[self.psb[2 + a]])
                for a in range(nb):
                    os_ = ocnt % 2
                    ocnt += 1
                    self.op("dve", lambda e, a=a, os_=os_: e.reciprocal(out=rd[os_][:], in_=self.ps[2 + a][:, 128:129]),
                            reads=[self.psb[2 + a]], writes=[rdB[os_]])
                    self.op("dve", lambda e, a=a, os_=os_: e.tensor_scalar(out=On[os_][:], in0=self.ps[2 + a][:, 0:128], scalar1=rd[os_][:, 0:1],
                                                                          scalar2=None, op0=ALU.mult),
                            reads=[self.psb[2 + a], rdB[os_]], writes=[OnB[os_]])
                    self.op("pe", lambda e, os_=os_: e.transpose(out=self.ps[6 + os_][:, 0:P], in_=On[os_][:], identity=self.ident),
                            reads=[OnB[os_], self.cfB], writes=[self.psb[6 + os_]])
                    self.op("act", lambda e, os_=os_: e.activation(out=oT[os_][:], in_=self.ps[6 + os_][:, 0:P], func=AF.Copy),
                            reads=[self.psb[6 + os_]], writes=[oTB[os_]])
                    blk = b0 + a
                    self.dma(d["OT"][h, :, blk * P:(blk + 1) * P], oT[os_][:], reads=[oTB[os_]], key=f"oT{os_}")

    def phase3(self, ph, prompt):
        c = self.c
        D, KC, NB, OWN, FH, SH, KV, T, HC, GRP = c["D"], c["KC"], c["NB"], c["OWN"], c["FH"], c["SH"], c["KV"], c["T"], c["HC"], c["GRP"]
        d = self.dram
        self.xcnt = 0
        self.tcnt = 0
        NT = 384 if prompt else 2 * T
        N = NT
        KVW = KV * 64
        HH = HC // 2
        L_ = {}
        xin = [self.sb(ph, [P, D], F32, "xin0")]; xinB = [Buf()]
        xin.append(xin[0]); xinB.append(xinB[0])
        xT = self.sb(ph, [P, KC, NT], F32, "xT"); xTB = Buf()
        hT = self.sb(ph, [P, KC, NT], BF16, "hT"); hTB = Buf()
        a2 = self.sb(ph, [P, HH, NT], BF16, "a2"); a2B = Buf()
        sq = [self.sb(ph, [P, NT], BF16, f"sq{i}") for i in range(2)]; sqB = [Buf(), Buf()]
        xr = [self.sb(ph, [P, NT], F32, f"xr{i}") for i in range(2)]; xrB = [Buf(), Buf()]
        rs = self.sb(ph, [P, NT], F32, "rs"); rsB = Buf()
        rl = [self.sb(ph, [P, NT], F32, f"rl{i}") for i in range(2)]; rlB = [Buf(), Buf()]
        qtm = self.sb(ph, [P, c["QKV"]], F32, "qtm"); qtmB = Buf()
        rt = [self.sb(ph, [P, SH, 8], F32, f"rt{i}") for i in range(4)]; rtB = [Buf() for _ in range(4)]
        rc = self.sb(ph, [P, SH, 8], F32, "rc"); rsn = self.sb(ph, [P, SH, 8], F32, "rsn"); rcB = Buf()
        QTa = self.sb(ph, [64, SH, P], BF16, "QTa"); QTaB = Buf()
        NCH = NT // 64 if prompt else 1
        KTa = self.sb(ph, [64, KV, 128 + max(NT, 64)], BF16, "KTa"); KTaB = Buf()
        Vc = self.sb(ph, [64, 2 + NCH, KV, 68], BF16, "Vc"); VcB = Buf()
        hal, halB = self.small["halom"]
        sk_, skB = self.small["sinkb"]
        esink = self.sb(ph, [P, SH], F32, "esink"); esB = Buf()
        self.op("act", lambda e: e.activation(out=esink[:], in_=sk_[:], func=AF.Exp), reads=[skB], writes=[esB])
        PTs = [self.sb(ph, [P, 512], BF16, f"PTs{i}") for i in range(3)]; PTsB = [Buf() for _ in range(3)]
        den = [self.sb(ph, [P, 4], F32, f"den{i}") for i in range(2)]; denB = [Buf(), Buf()]
        Ob = self.sb(ph, [P, max(GRP * 64, P) if prompt else D], F32, "Ob"); ObB = Buf()
        ystg = [self.sb(ph, [P, 512], F32, f"ystg{i}") for i in range(2)]; ystgB = [Buf(), Buf()]
        zb = self.sb(ph, [P, 1], F32, "zb")
        oTt = self.sb(ph, [P, KC, NT], BF16, "oTt"); oTtB = Buf()
        self.op("pool", lambda e: e.memset(KTa[:], 0.0), writes=[KTaB])
        self.op("pool", lambda e: e.memset(Vc[:], 0.0), writes=[VcB])
        self.op("pool", lambda e: e.memset(Vc[:, :, :, 64:65], 1.0), writes=[VcB])
        self.op("pool", lambda e: e.memset(zb[:], 0.0), writes=[esB])
        ycnt = [0]

        ntile = (OWN + 1) // 3 if prompt else 1
        nblk = 3 if prompt else 1
        rows = P if prompt else 2 * T
        segs = [(0, NT, 0)] if prompt else [(0, T, 1), (T, 2 * T, 2)]
        loads = []
        if not prompt:
            for base in (0, D, 2 * D, D):
                for gi in range(D // 512):
                    loads.append(self.wload_cols("wb_foxin", "foxin", base + gi * 512, 512))
        for _ in range(ntile):
            for l in range(2):
                nm = "foxout" if l == 0 else "swaout"
                if l == 1:
                    for b in range(nblk):
                        for gi in range(c["QKV"] // 512):
                            loads.append(self.wload_cols("wb_swain", "swain", gi * 512, 512))
                for gi in range(D // 512):
                    loads.append(self.wload_cols("wb_" + nm, nm, gi * 512, 512))
                ngu = c["DFF"] // 512
                for half in range(2):
                    for gi in range(ngu // 2):
                        loads.append(self.wload_cols("wb_up", f"up{l}", (half * (ngu // 2) + gi) * 512, 512, l))
                    for dc in range(KC):
                        loads.append(self.wload_dn(l, dc, half))
        wp = Builder.WPipe(self, self.wslots, self.wsb, loads)
        self.wi = 0
        mmc = [0]

        def wnext():
            w, wB = wp.get(self.wi)
            self.wi += 1
            return w, wB

        def proj_fm(rhsT, rhsB, nk, ngroups, evac, dn=False):
            for gi in range(ngroups):
                w, wB = wnext()
                if dn:
                    wv = w[:, 0:HH * P].rearrange("p (k n) -> p k n", n=P)
                    chunks = [(gi, lambda kc, wv=wv: wv[:, kc, :])]
                else:
                    wv = w[:, 0:KC * 512].rearrange("p (k n) -> p k n", n=512)
                    chunks = [(gi * 4 + i, (lambda kc, wv=wv, i=i: wv[:, kc, i * P:(i + 1) * P])) for i in range(4)]
                for ch, lf in chunks:
                    pb = 3 + (mmc[0] % 4)
                    mmc[0] += 1
                    for kc in range(nk):
                        self.op("pe", lambda e, kc=kc, pb=pb, lf=lf: e.matmul(out=self.ps[pb][:, 0:N], lhsT=lf(kc), rhs=rhsT[:, kc, 0:N],
                                                                             start=(kc == 0), stop=(kc == nk - 1)),
                                reads=[wB, rhsB], writes=[self.psb[pb]])
                    evac(ch, pb)

        def resid(l, gate):
            def ev(ch, pb):
                for (c0, c1, cd) in segs:
                    self.op("dve", lambda e, ch=ch, pb=pb, c0=c0, c1=c1, cd=cd: e.scalar_tensor_tensor(
                        out=xT[:, ch, c0:c1], in0=self.ps[pb][:, c0:c1], scalar=self.modcol(l, gate, ch, cd), in1=xT[:, ch, c0:c1],
                        op0=ALU.mult, op1=ALU.add), reads=[self.psb[pb], self.modB, xTB], writes=[xTB])
            return ev

        def ffn(l):
            self.norm_mod(xT, xTB, hT, hTB, N, segs, lambda kc, cd: self.modcol(l, "A2", kc, cd), lambda kc, cd: self.modcol(l, "B2", kc, cd),
                          sq, sqB, rs, rsB, xr, xrB)
            for half in range(2):
                def ev_up(ch, pb):
                    s = ch % 2
                    self.op("act", lambda e, pb=pb, s=s: e.activation(out=rl[s][:, 0:N], in_=self.ps[pb][:, 0:N], func=AF.Relu),
                            reads=[self.psb[pb]], writes=[rlB[s]])
                    self.op("pool", lambda e, ch=ch, s=s: e.tensor_tensor(out=a2[:, ch, 0:N], in0=rl[s][:, 0:N], in1=rl[s][:, 0:N], op=ALU.mult),
                            reads=[rlB[s]], writes=[a2B])
                proj_fm(hT, hTB, KC, c["DFF"] // 512 // 2, ev_up)
                proj_fm(a2, a2B, HH, KC, resid(l, "G2"), dn=True)

        L_.update(locals())
        for ti in range(ntile):
            L_["ti"] = ti
            if prompt:
                self.load_xT(xT, xTB, lambda b, ti=ti: d["xloc"][(ti * 3 + b) * P:(ti * 3 + b + 1) * P, :], 3, xin, xinB)
                self.dma(oTt[:, :, :], d["OT"][:, :, ti * NT:(ti + 1) * NT].rearrange("h p n -> p h n"), writes=[oTtB], key="oTt")
            else:
                self.load_xT(xT, xTB, lambda b: d["xs"][:, :], 1, xin, xinB, rows=2 * T)
                import os
                ks4 = int(os.environ.get("KS4", "9"))
                if ks4 == 0:
                    break
                self.fox_sample(L_, ph)
                if ks4 <= 3:
                    break
            proj_fm(oTt, oTtB, KC, D // 512, resid(0, "G1"))
            ffn(0)
            import os
            if (not prompt) and int(os.environ.get("KS4", "9")) == 4:
                break
            self.norm_mod(xT, xTB, hT, hTB, N, segs, lambda kc, cd: self.modcol(1, "A1", kc, cd), lambda kc, cd: self.modcol(1, "B1", kc, cd),
                          sq, sqB, rs, rsB, xr, xrB)
            if prompt and ti > 0:
                self.op("pool", lambda e: e.tensor_copy(out=KTa[:, :, 0:128], in_=KTa[:, :, NT:NT + 128]), reads=[KTaB], writes=[KTaB])
                self.op("pool", lambda e: e.tensor_copy(out=Vc[:, 0:2, :, :], in_=Vc[:, NCH:NCH + 2, :, :]), reads=[VcB], writes=[VcB])
            for b in range(nblk):
                for gi in range(c["QKV"] // 512):
                    w, wB = wnext()
                    wv = w[:, 0:KC * 512].rearrange("p (k n) -> p k n", n=512)
                    pb = 3 + (mmc[0] % 4)
                    mmc[0] += 1
                    for kc in range(KC):
                        self.op("pe", lambda e, kc=kc, pb=pb, wv=wv, b=b: e.matmul(
                            out=self.ps[pb][0:rows, 0:512], lhsT=hT[:, kc, b * P:b * P + rows], rhs=wv[:, kc, :], start=(kc == 0), stop=(kc == KC - 1)),
                            reads=[wB, hTB], writes=[self.psb[pb]])
                    self.op("act", lambda e, pb=pb, gi=gi: e.activation(out=qtm[0:rows, gi * 512:(gi + 1) * 512], in_=self.ps[pb][0:rows, 0:512], func=AF.Copy),
                            reads=[self.psb[pb]], writes=[qtmB])
                if prompt:
                    r0 = (ti * 3 + b) * P
                    self.dma(rc[0:rows], d["ropec"][r0:r0 + rows, :].rearrange("t (h i) -> t h i", i=8), writes=[rcB], key="rc")
                    self.dma(rsn[0:rows], d["ropes"][r0:r0 + rows, :].rearrange("t (h i) -> t h i", i=8), writes=[rcB], key="rc")
                else:
                    self.dma(rc[0:rows], d["ropecs"][:, :].rearrange("t (h i) -> t h i", i=8), writes=[rcB], key="rc")
                    self.dma(rsn[0:rows], d["ropess"][:, :].rearrange("t (h i) -> t h i", i=8), writes=[rcB], key="rc")
                for (c0, nh) in ((0, SH), (D, KV)):
                    v = qtm[0:rows, c0:c0 + nh * 64].rearrange("p (h n) -> p h n", n=64)
                    x1 = v[:, :, 0:8]
                    x2 = v[:, :, 8:16]
                    cc = rc[0:rows, 0:nh, :]
                    sn = rsn[0:rows, 0:nh, :]
                    for k_, (ia, ib) in enumerate(((x1, cc), (x2, sn), (x2, cc), (x1, sn))):
                        self.op("dve", lambda e, k_=k_, ia=ia, ib=ib, nh=nh: e.tensor_tensor(out=rt[k_][0:rows, 0:nh, :], in0=ia, in1=ib, op=ALU.mult),
                                reads=[qtmB, rcB], writes=[rtB[k_]])
                    self.op("dve", lambda e, x1=x1, nh=nh: e.tensor_tensor(out=x1, in0=rt[0][0:rows, 0:nh, :], in1=rt[1][0:rows, 0:nh, :], op=ALU.subtract),
                            reads=[rtB[0], rtB[1]], writes=[qtmB])
                    self.op("dve", lambda e, x2=x2, nh=nh: e.tensor_tensor(out=x2, in0=rt[2][0:rows, 0:nh, :], in1=rt[3][0:rows, 0:nh, :], op=ALU.add),
                            reads=[rtB[2], rtB[3]], writes=[qtmB])
                if prompt and ti == ntile - 1 and b == 2:
                    self.dma(d["sk"][:, :], qtm[:, D:D + KVW], reads=[qtmB], key="kvo")
                    self.dma(d["sv"][:, :], qtm[:, D + KVW:D + 2 * KVW], reads=[qtmB], key="kvo")
                if not prompt:
                    for s_ in range(2):
                        self.dma(d["sks"][s_, c["WB"] - T:c["WB"], :], qtm[s_ * T:(s_ + 1) * T, D:D + KVW], reads=[qtmB], key="kvo")
                        self.dma(d["svs"][s_, c["WB"] - T:c["WB"], :], qtm[s_ * T:(s_ + 1) * T, D + KVW:D + 2 * KVW], reads=[qtmB], key="kvo")
                        self.dma(d["sks"][s_, 0:c["WB"] - T, :], d["csk"][s_, T:c["WB"], :], key="kvo2")
                        self.dma(d["svs"][s_, 0:c["WB"] - T, :], d["csv"][s_, T:c["WB"], :], key="kvo2")
                tcol0 = b * P
                for h0 in range(0, SH + KV, 4):
                    pb = self.tcnt % 2
                    self.tcnt += 1
                    nh = min(4, SH + KV - h0)
                    for hh in range(nh):
                        h = h0 + hh
                        self.op("pe", lambda e, h=h, hh=hh, pb=pb: e.transpose(
                            out=self.ps[pb][0:64, hh * P:hh * P + rows], in_=qtm[0:rows, h * 64:(h + 1) * 64], identity=self.ident[0:rows, 0:rows]),
                            reads=[qtmB, self.cfB], writes=[self.psb[pb]])
                    src = self.ps[pb][0:64, 0:nh * P].rearrange("p (h n) -> p h n", n=P)[:, :, 0:rows]
                    if h0 < SH:
                        self.op("act", lambda e, h0=h0, nh=nh, src=src: e.activation(
                            out=QTa[:, h0:h0 + nh, 0:rows], in_=src, func=AF.Copy), reads=[self.psb[pb]], writes=[QTaB])
                    else:
                        self.op("act", lambda e, nh=nh, src=src, tcol0=tcol0: e.activation(
                            out=KTa[:, 0:nh, 128 + tcol0:128 + tcol0 + rows], in_=src, func=AF.Copy), reads=[self.psb[pb]], writes=[KTaB])
                if prompt:
                    for cc_ in range(2):
                        self.dma(Vc[:, 2 + b * 2 + cc_, :, 0:64], qtm[cc_ * 64:(cc_ + 1) * 64, D + KVW:D + 2 * KVW].rearrange("p (g n) -> p g n", n=64),
                                 reads=[qtmB], writes=[VcB], key="vcd", eng="pool")
                    self.swa_prompt(L_, b)
                else:
                    self.swa_sample(L_)
            if (not prompt) and int(os.environ.get("KS4", "9")) == 5:
                break
            proj_fm(oTt, oTtB, KC, D // 512, resid(1, "G1"))
            ffn(1)
            if (not prompt) and int(os.environ.get("KS4", "9")) == 6:
                break
            gfin, gfinB = self.small["gfinT"]
            self.norm_mod(xT, xTB, None, xTB, N, [(0, N, 0)], lambda kc, cd: gfin[:, kc:kc + 1], lambda kc, cd: self.zero,
                          sq, sqB, rs, rsB, xr, xrB, out_f32=xT)
            for b in range(nblk):
                if prompt and ti == 0 and b == 0:
                    continue
                for k0 in range(0, KC, 4):
                    pb = self.tcnt % 2
                    self.tcnt += 1
                    ys_ = ycnt[0] % 2
                    ycnt[0] += 1
                    for kk in range(4):
                        kc = k0 + kk
                        self.op("pe", lambda e, kc=kc, kk=kk, pb=pb, b=b: e.transpose(
                            out=self.ps[pb][0:rows, kk * P:(kk + 1) * P], in_=xT[:, kc, b * P:b * P + rows], identity=self.ident),
                            reads=[xTB, self.cfB], writes=[self.psb[pb]])
                    self.op("act", lambda e, pb=pb, ys_=ys_: e.activation(out=ystg[ys_][0:rows, :], in_=self.ps[pb][0:rows, 0:512], func=AF.Copy),
                            reads=[self.psb[pb]], writes=[ystgB[ys_]])
                    if prompt:
                        blk = ti * 3 + b - 1
                        self.dma(d["y"][blk * P:(blk + 1) * P, k0 * P:(k0 + 4) * P], ystg[ys_][:, :], reads=[ystgB[ys_]], key=f"ystg{ys_}")
                    else:
                        self.dma(d["ys"][:, k0 * P:(k0 + 4) * P], ystg[ys_][0:rows, :], reads=[ystgB[ys_]], key=f"ystg{ys_}")

    def swa_prompt(self, L_, b):
        c = self.c
        D, KC, SH, KV, GRP = c["D"], c["KC"], c["SH"], c["KV"], c["GRP"]
        QTa, QTaB, KTa, KTaB, Vc, VcB = L_["QTa"], L_["QTaB"], L_["KTa"], L_["KTaB"], L_["Vc"], L_["VcB"]
        PTs, PTsB, den, denB, Ob, ObB, esink, esB = L_["PTs"], L_["PTsB"], L_["den"], L_["denB"], L_["Ob"], L_["ObB"], L_["esink"], L_["esB"]
        oTt, oTtB, hal, halB, ti = L_["oTt"], L_["oTtB"], L_["hal"], L_["halB"], L_["ti"]
        zb = L_["zb"]
        sc = 1.0 / 8.0
        HPM = 2
        nchg = max(1, GRP * 64 // P)
        if not hasattr(self, "pc"):
            self.pc = 0
        for ql in range(2):
            qc = b * 2 + ql
            for g in range(KV):
                for hp in range(0, GRP, HPM):
                    heads = [g * GRP + hp + i for i in range(min(HPM, GRP - hp))]
                    nh = len(heads)
                    pts = []
                    for kci in range(3):
                        kc_ = qc - 2 + kci
                        kcol = 128 + 64 * kc_
                        sb_ = self.pc % 2
                        pp = self.pc % 3
                        self.pc += 1
                        self.op("pe", lambda e, g=g, kcol=kcol, heads=heads, nh=nh, ql=ql, sb_=sb_: e.matmul(
                            out=self.ps[sb_][0:64, 0:nh * 64], lhsT=KTa[:, g, kcol:kcol + 64], rhs=QTa[:, heads[0]:heads[0] + nh, ql * 64:(ql + 1) * 64],
                            start=True, stop=True), reads=[KTaB, QTaB], writes=[self.psb[sb_]])
                        use_hal = (ti == 0 and 0 <= kc_ < 2)
                        bias_ap = hal[0:64, 0:1] if use_hal else zb[0:64, 0:1]
                        self.op("act", lambda e, pp=pp, sb_=sb_, nh=nh, bias_ap=bias_ap: e.activation(
                            out=PTs[pp][0:64, 0:nh * 64], in_=self.ps[sb_][0:64, 0:nh * 64], func=AF.Exp, bias=bias_ap, scale=sc),
                            reads=[self.psb[sb_], halB, esB], writes=[PTsB[pp]])
                        pts.append((pp, 2 + kc_))
                    for i, h in enumerate(heads):
                        for kci, (pp, vslot) in enumerate(pts):
                            self.op("pe", lambda e, pp=pp, vslot=vslot, i=i, g=g, kci=kci: e.matmul(
                                out=self.ps[2][0:64, i * 65:(i + 1) * 65], lhsT=PTs[pp][0:64, i * 64:(i + 1) * 64], rhs=Vc[:, vslot, g, 0:65],
                                start=(kci == 0), stop=(kci == 2)), reads=[PTsB[pp], VcB], writes=[self.psb[2]])
                    ds_ = (self.pc // 3) % 2
                    ov = self.ps[2][0:64, 0:nh * 65].rearrange("p (h n) -> p h n", n=65)
                    self.op("dve", lambda e, ov=ov, nh=nh, ds_=ds_, h0=heads[0]: e.tensor_tensor(
                        out=den[ds_][0:64, 0:nh], in0=ov[:, :, 64], in1=esink[0:64, h0:h0 + nh], op=ALU.add),
                        reads=[self.psb[2], esB], writes=[denB[ds_]])
                    self.op("dve", lambda e, nh=nh, ds_=ds_: e.reciprocal(out=den[ds_][0:64, 0:nh], in_=den[ds_][0:64, 0:nh]),
                            reads=[denB[ds_]], writes=[denB[ds_]])
                    for i, h in enumerate(heads):
                        hl = h - g * GRP
                        self.op("dve", lambda e, i=i, hl=hl, ds_=ds_, ov=ov: e.tensor_scalar(
                            out=Ob[0:64, hl * 64:(hl + 1) * 64], in0=ov[:, i, 0:64], scalar1=den[ds_][0:64, i:i + 1], scalar2=None, op0=ALU.mult),
                            reads=[self.psb[2], denB[ds_]], writes=[ObB])
                pb = 6 + (self.tcnt % 2)
                self.tcnt += 1
                for kk in range(nchg):
                    self.op("pe", lambda e, kk=kk, pb=pb: e.transpose(
                        out=self.ps[pb][:, kk * 64:(kk + 1) * 64], in_=Ob[0:64, kk * P:(kk + 1) * P], identity=self.ident[0:64, 0:64]),
                        reads=[ObB, self.cfB], writes=[self.psb[pb]])
                self.op("act", lambda e, pb=pb, g=g, qc=qc: e.activation(
                    out=oTt[:, g * nchg:(g + 1) * nchg, qc * 64:(qc + 1) * 64], in_=self.ps[pb][:, 0:nchg * 64].rearrange("p (k n) -> p k n", n=64), func=AF.Copy),
                    reads=[self.psb[pb]], writes=[oTtB])

    def swa_sample(self, L_):
        c = self.c
        D, KC, SH, KV, GRP, T, WB = c["D"], c["KC"], c["SH"], c["KV"], c["GRP"], c["T"], c["WB"]
        d = self.dram
        KVW = KV * 64
        QTa, QTaB, KTa, KTaB = L_["QTa"], L_["QTaB"], L_["KTa"], L_["KTaB"]
        PTs, PTsB, den, denB, Ob, ObB, esink, esB = L_["PTs"], L_["PTsB"], L_["den"], L_["denB"], L_["Ob"], L_["ObB"], L_["esink"], L_["esB"]
        oTt, oTtB, qtm, qtmB, zb = L_["oTt"], L_["oTtB"], L_["qtm"], L_["qtmB"], L_["zb"]
        cin, cinB = L_["cin"], L_["cinB"]
        Vc, VcB = L_["Vc"], L_["VcB"]
        sc = 1.0 / 8.0
        for s_ in range(2):
            self.dma(cin[0][0:WB, 0:KVW], d["csk"][s_, :, :], writes=[cinB[0]], key="cin0")
            pb = self.tcnt % 2
            self.tcnt += 1
            for g in range(KV):
                self.op("pe", lambda e, g=g, pb=pb: e.transpose(out=self.ps[pb][0:64, g * P:g * P + WB], in_=cin[0][0:WB, g * 64:(g + 1) * 64],
                                                                  identity=self.ident[0:WB, 0:WB]), reads=[cinB[0], self.cfB], writes=[self.psb[pb]])
            self.op("act", lambda e, pb=pb: e.activation(out=KTa[:, 0:KV, 0:WB], in_=self.ps[pb][0:64, 0:KV * P].rearrange("p (g n) -> p g n", n=P)[:, :, 0:WB],
                                                         func=AF.Copy), reads=[self.psb[pb]], writes=[KTaB])
            for cc_ in range(WB // 64):
                self.dma(Vc[:, cc_, :, 0:64], d["csv"][s_, cc_ * 64:(cc_ + 1) * 64, :].rearrange("p (g n) -> p g n", n=64),
                         writes=[VcB], key="vcd", eng="pool")
            self.dma(Vc[0:T, 2, :, 0:64], qtm[s_ * T:(s_ + 1) * T, D + KVW:D + 2 * KVW].rearrange("p (g n) -> p g n", n=64),
                     reads=[qtmB], writes=[VcB], key="vcd", eng="pool")
            for g in range(KV):
                for hp in range(0, GRP, 4):
                    heads = [g * GRP + hp + i for i in range(min(4, GRP - hp))]
                    nh = len(heads)
                    pts = []
                    srcs = [(0, 64, 0), (64, 64, 1), (128 + s_ * T, T, 2)]
                    for kci, (kcol, nk, vslot) in enumerate(srcs):
                        sb_ = kci % 2
                        pp = kci
                        self.op("pe", lambda e, g=g, kcol=kcol, nk=nk, heads=heads, nh=nh, sb_=sb_, s_=s_: e.matmul(
                            out=self.ps[sb_][0:nk, 0:nh * T], lhsT=KTa[:, g, kcol:kcol + nk], rhs=QTa[:, heads[0]:heads[0] + nh, s_ * T:(s_ + 1) * T],
                            start=True, stop=True), reads=[KTaB, QTaB], writes=[self.psb[sb_]])
                        self.op("act", lambda e, pp=pp, sb_=sb_, nh=nh, nk=nk: e.activation(
                            out=PTs[pp][0:nk, 0:nh * T], in_=self.ps[sb_][0:nk, 0:nh * T], func=AF.Exp, bias=zb[0:nk, 0:1], scale=sc),
                            reads=[self.psb[sb_], esB], writes=[PTsB[pp]])
                        pts.append((pp, nk, vslot))
                    for i, h in enumerate(heads):
                        for kci, (pp, nk, vslot) in enumerate(pts):
                            self.op("pe", lambda e, pp=pp, nk=nk, vslot=vslot, i=i, g=g, kci=kci: e.matmul(
                                out=self.ps[2][0:T, i * 65:(i + 1) * 65], lhsT=PTs[pp][0:nk, i * T:(i + 1) * T], rhs=Vc[0:nk, vslot, g, 0:65],
                                start=(kci == 0), stop=(kci == 2)), reads=[PTsB[pp], VcB], writes=[self.psb[2]])
                    ov = self.ps[2][0:T, 0:nh * 65].rearrange("p (h n) -> p h n", n=65)
                    ds_ = (hp // 4) % 2
                    self.op("dve", lambda e, ov=ov, nh=nh, ds_=ds_, h0=heads[0]: e.tensor_tensor(
                        out=den[ds_][0:T, 0:nh], in0=ov[:, :, 64], in1=esink[0:T, h0:h0 + nh], op=ALU.add),
                        reads=[self.psb[2], esB], writes=[denB[ds_]])
                    self.op("dve", lambda e, nh=nh, ds_=ds_: e.reciprocal(out=den[ds_][0:T, 0:nh], in_=den[ds_][0:T, 0:nh]),
                            reads=[denB[ds_]], writes=[denB[ds_]])
                    for i, h in enumerate(heads):
                        self.op("dve", lambda e, i=i, h=h, ds_=ds_, ov=ov: e.tensor_scalar(
                            out=Ob[0:T, h * 64:(h + 1) * 64], in0=ov[:, i, 0:64], scalar1=den[ds_][0:T, i:i + 1], scalar2=None, op0=ALU.mult),
                            reads=[self.psb[2], denB[ds_]], writes=[ObB])
            for k0 in range(0, KC, 4):
                pb = 6 + (self.tcnt % 2)
                self.tcnt += 1
                for kk in range(4):
                    kc = k0 + kk
                    self.op("pe", lambda e, kc=kc, kk=kk, pb=pb: e.transpose(
                        out=self.ps[pb][:, kk * T:(kk + 1) * T], in_=Ob[0:T, kc * P:(kc + 1) * P], identity=self.ident[0:T, 0:T]),
                        reads=[ObB, self.cfB], writes=[self.psb[pb]])
                self.op("act", lambda e, k0=k0, pb=pb, s_=s_: e.activation(
                    out=oTt[:, k0:k0 + 4, s_ * T:(s_ + 1) * T], in_=self.ps[pb][:, 0:4 * T].rearrange("p (k n) -> p k n", n=T), func=AF.Copy),
                    reads=[self.psb[pb]], writes=[oTtB])

    def fox_sample(self, L_, ph):
        c = self.c
        D, KC, FH, T, PB, PAST = c["D"], c["KC"], c["FH"], c["T"], c["PB"], c["PAST"]
        d = self.dram
        xT, xTB, hT, hTB = L_["xT"], L_["xTB"], L_["hT"], L_["hTB"]
        sq, sqB, rs, rsB, xr, xrB = L_["sq"], L_["sqB"], L_["rs"], L_["rsB"], L_["xr"], L_["xrB"]
        oTt, oTtB, PTs, PTsB, Ob, ObB, den, denB = L_["oTt"], L_["oTtB"], L_["PTs"], L_["PTsB"], L_["Ob"], L_["ObB"], L_["den"], L_["denB"]
        ystg, ystgB, ycnt = L_["ystg"], L_["ystgB"], L_["ycnt"]
        wnext = L_["wnext"]
        mmc = L_["mmc"]
        KcT = self.sb(ph, [P, FH, PAST], BF16, "KcT"); KcTB = Buf()
        Vcs = self.sb(ph, [P, PB, FH, 132], BF16, "Vcs"); VcsB = Buf()
        cin = [L_["xin"][0], L_["xin"][0]]; cinB = [L_["xinB"][0], L_["xinB"][0]]
        L_["cin"] = cin; L_["cinB"] = cinB
        lfc = self.sb(ph, [P, PB, FH], F32, "lfc"); lfcB = Buf()
        bc = self.sb(ph, [P, PB + 1, FH], F32, "bc"); bcB = Buf()
        tmpf = self.sb(ph, [P, PB + 1, FH], F32, "tmpf"); tmpfB = Buf()
        lfn = self.sb(ph, [P, FH], F32, "lfn"); lfnB = Buf()
        lfn2 = self.sb(ph, [P, FH], F32, "lfn2"); lfn2B = Buf()
        QTn = self.sb(ph, [P, FH, 2 * T], BF16, "QTn"); KTn = self.sb(ph, [P, FH, 2 * T], BF16, "KTn"); QKnB = Buf()
        Vn = self.sb(ph, [P, FH, 132], BF16, "Vn"); VnB = Buf()
        Vn1 = self.sb(ph, [P, FH, 132], BF16, "Vn1"); Vn1B = Buf()
        wf = self.sb(ph, [P, KC, FH], BF16, "wfs"); wfB = Buf()
        self.op("pool", lambda e: e.memset(Vcs[:], 1.0), writes=[VcsB])
        self.op("pool", lambda e: e.memset(Vn[:], 1.0), writes=[VnB])
        N = 2 * T
        segs = [(0, T, 1), (T, 2 * T, 2)]
        scale = 1.0 / np.sqrt(128.0)
        bf, bfb = self.small["fox_bfb"]
        self.norm_mod(xT, xTB, hT, hTB, N, segs, lambda kc, cd: self.modcol(0, "A1", kc, cd), lambda kc, cd: self.modcol(0, "B1", kc, cd),
                      sq, sqB, rs, rsB, xr, xrB)
        for kind, dst in (("q", QTn), ("k", KTn)):
            for gi in range(D // 512):
                w, wB = wnext()
                wv = w[:, 0:KC * 512].rearrange("p (k n) -> p k n", n=512)
                for i4 in range(4):
                    pb = 3 + (mmc[0] % 4)
                    mmc[0] += 1
                    for kc in range(KC):
                        self.op("pe", lambda e, kc=kc, i4=i4, pb=pb, wv=wv: e.matmul(out=self.ps[pb][:, 0:N], lhsT=wv[:, kc, i4 * P:(i4 + 1) * P],
                                                                                   rhs=hT[:, kc, 0:N], start=(kc == 0), stop=(kc == KC - 1)),
                                reads=[wB, hTB], writes=[self.psb[pb]])
                    self.op("act", lambda e, pb=pb, dst=dst, hd=gi * 4 + i4: e.activation(out=dst[:, hd, :], in_=self.ps[pb][:, 0:N], func=AF.Copy),
                            reads=[self.psb[pb]], writes=[QKnB])
        for kind in ("v", "kt"):
            for gi in range(D // 512):
                w, wB = wnext()
                wv = w[:, 0:KC * 512].rearrange("p (k n) -> p k n", n=512)
                pb = 3 + (mmc[0] % 4)
                mmc[0] += 1
                ys_ = ycnt[0] % 2
                ycnt[0] += 1
                for kc in range(KC):
                    self.op("pe", lambda e, kc=kc, pb=pb, wv=wv: e.matmul(out=self.ps[pb][0:N, 0:512], lhsT=hT[:, kc, 0:N], rhs=wv[:, kc, :],
                                                                         start=(kc == 0), stop=(kc == KC - 1)), reads=[wB, hTB], writes=[self.psb[pb]])
                self.op("act", lambda e, pb=pb, ys_=ys_: e.activation(out=ystg[ys_][0:N, :], in_=self.ps[pb][0:N, 0:512], func=AF.Copy),
                        reads=[self.psb[pb]], writes=[ystgB[ys_]])
                if kind == "v":
                    self.op("dve", lambda e, ys_=ys_, gi=gi: e.tensor_copy(out=Vn[0:N, gi * 4:(gi + 1) * 4, 0:128],
                                                                        in_=ystg[ys_][0:N, :].rearrange("p (h n) -> p h n", n=128)),
                            reads=[ystgB[ys_]], writes=[VnB])
                self.dma(d["fvs" if kind == "v" else "fks"][:, gi * 512:(gi + 1) * 512], ystg[ys_][0:N, :], reads=[ystgB[ys_]], key=f"ystg{ys_}")
        import os
        ks4 = int(os.environ.get("KS4", "9"))
        if ks4 == 1:
            return
        self.dma(Vn1[0:T, :, :], Vn[T:2 * T, :, :], reads=[VnB], writes=[Vn1B], key="vn1")
        self.dma(wf[:], d["wb_wf"][:, :, :], reads=self.wbufs["foxin"], writes=[wfB], key="c_wf2")
        for kc in range(KC):
            self.op("pe", lambda e, kc=kc: e.matmul(out=self.ps[7][0:N, 0:FH], lhsT=hT[:, kc, 0:N], rhs=wf[:, kc, :], start=(kc == 0), stop=(kc == KC - 1)),
                    reads=[hTB, wfB], writes=[self.psb[7]])
        self.op("dve", lambda e: e.tensor_tensor(out=lfn[0:N, :], in0=self.ps[7][0:N, 0:FH], in1=bf[0:N, :], op=ALU.add), reads=[self.psb[7], bfb], writes=[lfnB])
        self.op("act", lambda e: e.activation(out=lfn[0:N, :], in_=lfn[0:N, :], func=AF.Exp, scale=-1.0), reads=[lfnB], writes=[lfnB])
        self.op("act", lambda e: e.activation(out=lfn[0:N, :], in_=lfn[0:N, :], func=AF.Ln, bias=self.one[0:N, :], scale=1.0), reads=[lfnB, self.cstB], writes=[lfnB])
        self.op("dve", lambda e: e.tensor_scalar(out=lfn[0:N, :], in0=lfn[0:N, :], scalar1=-1.0, scalar2=None, op0=ALU.mult), reads=[lfnB], writes=[lfnB])
        self.dma(d["fls"][:, :], lfn[0:N, :], reads=[lfnB], key="lfn")
        for s_ in range(2):
            Vns = Vn if s_ == 0 else Vn1
            VnsB = VnB if s_ == 0 else Vn1B
            self.dma(lfn2[0:T, :], lfn[s_ * T:(s_ + 1) * T, :], reads=[lfnB], writes=[lfn2B], key="lfn2")
            for jb in range(PB):
                cs = jb % 2
                self.dma(cin[cs][:], d["cfk"][s_, jb * P:(jb + 1) * P, :], writes=[cinB[cs]], key=f"cin{cs}")
                for h0 in range(0, FH, 4):
                    pb = self.tcnt % 2
                    self.tcnt += 1
                    for hh in range(4):
                        self.op("pe", lambda e, h=h0 + hh, hh=hh, pb=pb, cs=cs: e.transpose(out=self.ps[pb][:, hh * P:(hh + 1) * P], in_=cin[cs][:, h * P:(h + 1) * P],
                                                                                         identity=self.ident), reads=[cinB[cs], self.cfB], writes=[self.psb[pb]])
                    self.op("act", lambda e, h0=h0, pb=pb, jb=jb: e.activation(out=KcT[:, h0:h0 + 4, jb * P:(jb + 1) * P],
                                                                              in_=self.ps[pb][:, 0:512].rearrange("p (h n) -> p h n", n=P), func=AF.Copy),
                            reads=[self.psb[pb]], writes=[KcTB])
                self.dma(Vcs[:, jb, :, 0:128], d["cfv"][s_, jb * P:(jb + 1) * P, :].rearrange("p (h n) -> p h n", n=128), writes=[VcsB], key="vcd", eng="pool")
            self.dma(lfc[:], d["cfl"][s_].rearrange("(j p) h -> p j h", p=P), writes=[lfcB], key="lfc")
            lf2 = lfc[:].rearrange("p j h -> p (j h)")
            self.op("pe", lambda e: e.matmul(out=self.ps[6][:, 0:PB * FH], lhsT=self.Lgt, rhs=lf2, start=True, stop=True), reads=[lfcB, self.cfB], writes=[self.psb[6]])
            self.op("pe", lambda e: e.matmul(out=self.ps[7][:, 0:PB * FH], lhsT=self.ones32, rhs=lf2, start=True, stop=True), reads=[lfcB, self.cfB], writes=[self.psb[7]])
            self.op("dve", lambda e: e.tensor_copy(out=tmpf[:, 0:PB, :], in_=self.ps[7][:, 0:PB * FH].rearrange("p (j h) -> p j h", h=FH)), reads=[self.psb[7]], writes=[tmpfB])
            self.op("pe", lambda e: e.matmul(out=self.ps[7][:, 0:FH], lhsT=self.ones32[0:T, :], rhs=lfn2[0:T, :], start=True, stop=True),
                    reads=[lfn2B, self.cfB], writes=[self.psb[7]])
            self.op("dve", lambda e: e.tensor_copy(out=tmpf[:, PB, :], in_=self.ps[7][:, 0:FH]), reads=[self.psb[7]], writes=[tmpfB])
            self.op("pe", lambda e: e.matmul(out=self.ps[7][0:T, FH:2 * FH], lhsT=self.Lgt[0:T, 0:T], rhs=lfn2[0:T, :], start=True, stop=True),
                    reads=[lfn2B, self.cfB], writes=[self.psb[7]])
            self.op("dve", lambda e: e.tensor_copy(out=bc[0:T, PB, :], in_=self.ps[7][0:T, FH:2 * FH]), reads=[self.psb[7]], writes=[bcB])
            for j in range(PB - 1, -1, -1):
                self.op("dve", lambda e, j=j: e.tensor_tensor(out=bc[:, j, :], in0=self.ps[6][:, j * FH:(j + 1) * FH], in1=tmpf[:, PB, :], op=ALU.add),
                        reads=[self.psb[6], tmpfB], writes=[bcB])
                if j > 0:
                    self.op("dve", lambda e, j=j: e.tensor_tensor(out=tmpf[:, PB, :], in0=tmpf[:, PB, :], in1=tmpf[:, j, :], op=ALU.add),
                            reads=[tmpfB], writes=[tmpfB])
            if ks4 == 2:
                continue
            for h in range(FH):
                sb_ = h % 2
                pp = h % 3
                for jb in range(PB):
                    self.op("pe", lambda e, h=h, jb=jb, sb_=sb_, s_=s_: e.matmul(out=self.ps[sb_][:, jb * T:(jb + 1) * T], lhsT=KcT[:, h, jb * P:(jb + 1) * P],
                                                                                rhs=QTn[:, h, s_ * T:(s_ + 1) * T], start=True, stop=True),
                            reads=[KcTB, QKnB], writes=[self.psb[sb_]])
                self.op("pe", lambda e, h=h, sb_=sb_, s_=s_: e.matmul(out=self.ps[sb_][0:T, PB * T:(PB + 1) * T], lhsT=KTn[:, h, s_ * T:(s_ + 1) * T],
                                                                     rhs=QTn[:, h, s_ * T:(s_ + 1) * T], start=True, stop=True),
                        reads=[QKnB], writes=[self.psb[sb_]])
                for jb in range(PB):
                    self.op("act", lambda e, h=h, jb=jb, sb_=sb_, pp=pp: e.activation(out=PTs[pp][:, jb * T:(jb + 1) * T], in_=self.ps[sb_][:, jb * T:(jb + 1) * T],
                                                                                     func=AF.Exp, bias=bc[:, jb, h:h + 1], scale=float(scale)),
                            reads=[self.psb[sb_], bcB], writes=[PTsB[pp]])
                self.op("act", lambda e, h=h, sb_=sb_, pp=pp: e.activation(out=PTs[pp][0:T, PB * T:(PB + 1) * T], in_=self.ps[sb_][0:T, PB * T:(PB + 1) * T],
                                                                          func=AF.Exp, bias=bc[0:T, PB, h:h + 1], scale=float(scale)),
                        reads=[self.psb[sb_], bcB], writes=[PTsB[pp]])
                self.op("pool", lambda e, pp=pp: e.tensor_tensor(out=PTs[pp][0:T, PB * T:(PB + 1) * T], in0=PTs[pp][0:T, PB * T:(PB + 1) * T],
                                                                 in1=self.tri[0:T, 0:T], op=ALU.mult), reads=[PTsB[pp], self.cbB], writes=[PTsB[pp]])
                for jb in range(PB):
                    self.op("pe", lambda e, h=h, jb=jb, pp=pp: e.matmul(out=self.ps[2][0:T, 0:129], lhsT=PTs[pp][:, jb * T:(jb + 1) * T], rhs=Vcs[:, jb, h, 0:129],
                                                                       start=(jb == 0), stop=False), reads=[PTsB[pp], VcsB], writes=[self.psb[2]])
                self.op("pe", lambda e, h=h, pp=pp, Vns=Vns: e.matmul(out=self.ps[2][0:T, 0:129], lhsT=PTs[pp][0:T, PB * T:(PB + 1) * T], rhs=Vns[0:T, h, 0:129],
                                                                     start=False, stop=True), reads=[PTsB[pp], VnsB], writes=[self.psb[2]])
                ds_ = h % 2
                self.op("dve", lambda e, ds_=ds_: e.reciprocal(out=den[ds_][0:T, 0:1], in_=self.ps[2][0:T, 128:129]), reads=[self.psb[2]], writes=[denB[ds_]])
                self.op("dve", lambda e, ds_=ds_, h=h: e.tensor_scalar(out=Ob[0:T, h * P:(h + 1) * P], in0=self.ps[2][0:T, 0:128], scalar1=den[ds_][0:T, 0:1],
                                                                       scalar2=None, op0=ALU.mult), reads=[self.psb[2], denB[ds_]], writes=[ObB])
            for k0 in range(0, KC, 4):
                pb = 6 + (self.tcnt % 2)
                self.tcnt += 1
                for kk in range(4):
                    kc = k0 + kk
                    self.op("pe", lambda e, kc=kc, kk=kk, pb=pb: e.transpose(out=self.ps[pb][:, kk * T:(kk + 1) * T], in_=Ob[0:T, kc * P:(kc + 1) * P],
                                                                           identity=self.ident[0:T, 0:T]), reads=[ObB, self.cfB], writes=[self.psb[pb]])
                self.op("act", lambda e, k0=k0, pb=pb, s_=s_: e.activation(out=oTt[:, k0:k0 + 4, s_ * T:(s_ + 1) * T],
                                                                          in_=self.ps[pb][:, 0:4 * T].rearrange("p (k n) -> p k n", n=T), func=AF.Copy),
                        reads=[self.psb[pb]], writes=[oTtB])


def _fm(v, nchunk):
    return np.ascontiguousarray(v.reshape(nchunk, P).T)


def prepare_inputs(cfg, inp, core):
    c = cfg
    D, S, T, KC, NB, OWN, FH, SH, KV, L = c["D"], c["S"], c["T"], c["KC"], c["NB"], c["OWN"], c["FH"], c["SH"], c["KV"], c["L"]
    b, r = core // 4, core % 4
    f = np.float32
    xp = inp["x_prompt"][b]
    own0 = r * OWN
    order = []
    valid = []
    order.append(own0 - 1 if r > 0 else 0); valid.append(r > 0)
    for j in range(OWN):
        order.append(own0 + j); valid.append(True)
    if r == 0:
        oth = list(range(OWN, OWN + c["NO"]))
        ov = [True] * c["NO"]
    else:
        oth = [j for j in range(NB) if not (own0 - 1 <= j < own0 + OWN)]
        ov = [True] * len(oth)
    order += oth; valid += ov
    assert len(order) == NB
    xloc = np.empty((NB * P, D), f)
    for li, tj in enumerate(order):
        if valid[li]:
            xloc[li * P:(li + 1) * P] = xp[tj * P:(tj + 1) * P]
        else:
            xloc[li * P:(li + 1) * P] = 0.0
    m = {"xloc": xloc}
    m["xs"] = np.ascontiguousarray(inp["x_sample"][2 * core:2 * core + 2].reshape(2 * T, D))
    cv = np.stack([inp["c_prompt"][b], inp["c_sample"][2 * core], inp["c_sample"][2 * core + 1]], 0)
    m["cT"] = np.ascontiguousarray(cv.reshape(3, KC, P).transpose(2, 1, 0))
    m["ada_w"] = inp["ada_w"]
    m["ada_bT"] = np.ascontiguousarray(inp["ada_b"].reshape(L, 6 * KC, P).transpose(2, 0, 1))
    m["gmixT"] = np.ascontiguousarray(inp["norm_mix_g"].reshape(L, KC, P).transpose(2, 0, 1))
    m["gffnT"] = np.ascontiguousarray(inp["norm_ffn_g"].reshape(L, KC, P).transpose(2, 0, 1))
    m["gfinT"] = _fm(inp["final_g"], KC)
    m["fox_w_in"] = inp["fox_w_in"][0]
    m["fox_bfb"] = np.ascontiguousarray(np.broadcast_to(inp["fox_b_f"][0][None, :], (P, FH))).astype(f)
    m["fox_w_out"] = inp["fox_w_out"][0]
    m["swa_w_in"] = inp["swa_w_in"][0]
    m["sinkb"] = np.ascontiguousarray(np.broadcast_to(inp["swa_sinks"][0][None, :], (P, SH))).astype(f)
    m["swa_w_out"] = inp["swa_w_out"][0]
    m["w_up"] = inp["ffn_w_up"]
    m["w_dn"] = inp["ffn_w_down"]
    m["cfk"] = np.ascontiguousarray(inp["cache_fox_k"][0, 2 * core:2 * core + 2].reshape(2, c["PAST"], D))
    m["cfv"] = np.ascontiguousarray(inp["cache_fox_v"][0, 2 * core:2 * core + 2].reshape(2, c["PAST"], D))
    m["cfl"] = np.ascontiguousarray(inp["cache_fox_logf"][0, 2 * core:2 * core + 2])
    m["csk"] = np.ascontiguousarray(inp["cache_swa_k"][0, 2 * core:2 * core + 2].reshape(2, c["WB"], KV * 64))
    m["csv"] = np.ascontiguousarray(inp["cache_swa_v"][0, 2 * core:2 * core + 2].reshape(2, c["WB"], KV * 64))
    cf = np.zeros((P, 6 * P), f)
    ii = np.arange(P)
    cf[:, 0:P] = np.eye(P)
    cf[:, P:2 * P] = (ii[:, None] <= ii[None, :])
    cf[:, 2 * P:3 * P] = (ii[:, None] > ii[None, :])
    cf[:, 3 * P:4 * P] = 1.0
    cf[127, 4 * P:5 * P] = 1.0
    slp = np.zeros((P, P), f)
    for a in range(NB):
        for bb in range(NB):
            if valid[a] and valid[bb] and order[a] < order[bb] and not (r == 0 and a >= OWN + 1) and not (r == 0 and bb >= OWN + 1):
                slp[a, bb] = 1.0
    cf[:, 5 * P:6 * P] = slp
    m["cf"] = cf
    GL = c["GL"]
    bm = np.zeros((P, len(GL), NB), f)
    for gi, (b0, nb) in enumerate(GL):
        for j in range(NB):
            if j <= OWN:
                if j == 0 and gi > 0 and r == 0:
                    bm[:, gi, j] = NEG
            else:
                vis = (r > 0) and (order[j] < own0)
                if not vis:
                    bm[:, gi, j] = NEG
    m["bmb"] = bm
    m["halom"] = np.full((P, 1), NEG if r == 0 else 0.0, f)
    half = 8
    inv = (500000.0 ** (-np.arange(half, dtype=np.float32) * 2.0 / 16.0)).astype(f)
    pos = (own0 * P - P + np.arange((OWN + 1) * P)).astype(f)
    pos = np.maximum(pos, 0.0)
    ang = pos[:, None] * inv[None, :]
    m["ropec"] = np.ascontiguousarray(np.tile(np.cos(ang).astype(f), (1, SH)))
    m["ropes"] = np.ascontiguousarray(np.tile(np.sin(ang).astype(f), (1, SH)))
    poss = (c["PAST"] + np.arange(T)).astype(f)
    angs = np.tile(poss[:, None] * inv[None, :], (2, 1))
    m["ropecs"] = np.ascontiguousarray(np.tile(np.cos(angs).astype(f), (1, SH)))
    m["ropess"] = np.ascontiguousarray(np.tile(np.sin(angs).astype(f), (1, SH)))
    return m


_NC_CACHE = {}


def kernel(**inp):
    inp = {k: np.asarray(v) for k, v in inp.items()}
    B, S, D = inp["x_prompt"].shape
    DB, T, _ = inp["x_sample"].shape
    PAST = inp["cache_fox_k"].shape[2]
    WB = inp["cache_swa_k"].shape[2]
    assert B == 2 and DB == 16
    cfg = make_cfg(D, S, T, PAST, WB)
    key = (D, S, T, PAST, WB)
    if key not in _NC_CACHE:
        _NC_CACHE[key] = Builder(cfg).build()
    nc = _NC_CACHE[key]
    in_maps = [prepare_inputs(cfg, inp, core) for core in range(8)]
    res = run_bass_kernel_spmd(nc, in_maps, core_ids=list(range(8)))
    R = res.results
    OWN, FH, KV = cfg["OWN"], cfg["FH"], cfg["KV"]
    f = np.float32
    y = np.empty((2, S, D), f); fk = np.empty((1, 2, S, FH, 128), f); fv = np.empty_like(fk); fl = np.empty((1, 2, S, FH), f)
    ys = np.empty((16, T, D), f); fks = np.empty((1, 16, T, FH, 128), f); fvs = np.empty_like(fks); fls = np.empty((1, 16, T, FH), f)
    sk = np.empty((1, 2, WB, KV, 64), f); sv = np.empty_like(sk)
    sks = np.empty((1, 16, WB, KV, 64), f); svs = np.empty_like(sks)
    n = OWN * P
    for core in range(8):
        b, r = core // 4, core % 4
        o = R[core]
        y[b, r * n:(r + 1) * n] = o["y"]
        fk[0, b, r * n:(r + 1) * n] = o["fk"].reshape(n, FH, 128)
        fv[0, b, r * n:(r + 1) * n] = o["fv"].reshape(n, FH, 128)
        fl[0, b, r * n:(r + 1) * n] = o["fl"]
        ys[2 * core:2 * core + 2] = o["ys"].reshape(2, T, D)
        fks[0, 2 * core:2 * core + 2] = o["fks"].reshape(2, T, FH, 128)
        fvs[0, 2 * core:2 * core + 2] = o["fvs"].reshape(2, T, FH, 128)
        fls[0, 2 * core:2 * core + 2] = o["fls"].reshape(2, T, FH)
        if r == 3:
            sk[0, b] = o["sk"].reshape(WB, KV, 64)
            sv[0, b] = o["sv"].reshape(WB, KV, 64)
        sks[0, 2 * core:2 * core + 2] = o["sks"].reshape(2, WB, KV, 64)
        svs[0, 2 * core:2 * core + 2] = o["svs"].reshape(2, WB, KV, 64)
    return (y, ys, fk, fv, fl, fks, fvs, fls, sk, sv, sks, svs)
```
